# Optimizing a Trainium2 kernel written in Bass

```python
import jax, jax.numpy as jnp
from jax import lax
import numpy as np

D_MODEL = 1024
BATCH = 8
SEQ = 4096
DEPTH = 2
DEC_BATCH = 16
DEC_SEQ = 4096
PAST_LEN = 128

A_HEADS = 16
A_KV_HEADS = 4
A_HEAD_DIM = D_MODEL // A_HEADS
A_WIDTH = A_HEADS * A_HEAD_DIM
A_KV_WIDTH = A_KV_HEADS * A_HEAD_DIM
WINDOW = 128
BLOCK = 128
A_SPLITS = (A_WIDTH, A_KV_WIDTH, A_KV_WIDTH, A_WIDTH)

G_HEADS = 4
G_KEY_WIDTH = D_MODEL // 2
G_VAL_WIDTH = D_MODEL
G_KEY_DIM = G_KEY_WIDTH // G_HEADS
G_VAL_DIM = G_VAL_WIDTH // G_HEADS
G_GATE_RANK = 16
G_GATE_TAU = 16.0
G_CHUNK = 64
G_SPLITS = (G_KEY_WIDTH, G_KEY_WIDTH, G_VAL_WIDTH, G_VAL_WIDTH, G_GATE_RANK, G_GATE_RANK)

LN_EPS = 1e-5
RMS_EPS = 1e-6
DN_ALPHA = (2 * DEPTH) ** 0.25
DN_BETA = (8 * DEPTH) ** -0.25

kernel_name = "hybrid_swa_gla_deepnorm_encoder"


def _split(h, sizes):
    idx = [int(i) for i in np.cumsum(sizes)[:-1]]
    return jnp.split(h, idx, axis=-1)


def layer_norm(x, g, b):
    xf = x.astype(jnp.float32)
    mu = jnp.mean(xf, axis=-1, keepdims=True)
    var = jnp.mean(jnp.square(xf - mu), axis=-1, keepdims=True)
    y = (xf - mu) * lax.rsqrt(var + LN_EPS) * g.astype(jnp.float32) + b.astype(jnp.float32)
    return y.astype(x.dtype)


def alibi_slopes(n_heads):
    return jnp.asarray(2.0 ** (-8.0 * np.arange(1, n_heads + 1) / n_heads), dtype=jnp.float32)


def window_attention(q, k, v, sink):
    B, S, H, dh = q.shape
    KV = k.shape[2]
    rep = H // KV
    nb = S // BLOCK
    qb = (q * dh ** -0.5).reshape(B, nb, BLOCK, KV, rep, dh)
    pad = ((0, 0), (BLOCK, BLOCK), (0, 0), (0, 0))
    kp = jnp.pad(k, pad).reshape(B, nb + 2, BLOCK, KV, dh)
    vp = jnp.pad(v, pad).reshape(B, nb + 2, BLOCK, KV, dh)
    kw = jnp.concatenate([kp[:, :-2], kp[:, 1:-1], kp[:, 2:]], axis=2)
    vw = jnp.concatenate([vp[:, :-2], vp[:, 1:-1], vp[:, 2:]], axis=2)
    qi = jnp.arange(BLOCK)[:, None] + BLOCK
    kj = jnp.arange(3 * BLOCK)[None, :]
    dist = jnp.abs(qi - kj).astype(jnp.float32)
    in_win = dist <= WINDOW
    kpos = (jnp.arange(nb)[:, None] - 1) * BLOCK + jnp.arange(3 * BLOCK)[None, :]
    kvalid = (kpos >= 0) & (kpos < S)
    slopes = alibi_slopes(H).reshape(KV, rep)
    bias = -slopes[:, :, None, None] * dist
    sink_l = sink.astype(jnp.float32).reshape(KV, rep)

    def one_block(args):
        qn, kn, vn, valid = args
        s = jnp.einsum('bqgrd,bkgd->bgrqk', qn, kn).astype(jnp.float32) + bias
        s = jnp.where(in_win & valid[None, :], s, -1e30)
        sk = jnp.broadcast_to(sink_l[None, :, :, None, None], s.shape[:-1] + (1,))
        p = jax.nn.softmax(jnp.concatenate([s, sk], axis=-1), axis=-1)[..., :-1]
        return jnp.einsum('bgrqk,bkgd->bqgrd', p.astype(vn.dtype), vn)

    out = lax.map(one_block, (jnp.moveaxis(qb, 1, 0), jnp.moveaxis(kw, 1, 0),
                              jnp.moveaxis(vw, 1, 0), kvalid))
    return jnp.moveaxis(out, 0, 1).reshape(B, S, H * dh)


def gla_direction(q, k, v, log_a, inclusive):
    B, S, H, dk = q.shape
    dv = v.shape[-1]
    nc = S // G_CHUNK

    def chunks(t):
        return jnp.moveaxis(t.reshape(B, nc, G_CHUNK, H, t.shape[-1]), 1, 0)

    qc, kc, vc = chunks(q), chunks(k), chunks(v)
    cum = jnp.cumsum(chunks(log_a.astype(jnp.float32)), axis=2)
    idx = jnp.arange(G_CHUNK)
    mask = (idx[:, None] >= idx[None, :]) if inclusive else (idx[:, None] > idx[None, :])

    def step(state, xs):
        qn, kn, vn, bn = xs
        qf, kf, vf = qn.astype(jnp.float32), kn.astype(jnp.float32), vn.astype(jnp.float32)
        b_last = bn[:, -1]
        q_dec = qf * jnp.exp(bn)
        k_inv = kf * jnp.exp(-bn)
        k_tail = kf * jnp.exp(b_last[:, None] - bn)
        inter = jnp.einsum('bthk,bhkv->bthv', q_dec, state)
        att = jnp.where(mask, jnp.einsum('bthk,bshk->bhts', q_dec, k_inv), 0.0)
        intra = jnp.einsum('bhts,bshv->bthv', att, vf)
        new_state = jnp.exp(b_last)[..., None] * state + jnp.einsum('bshk,bshv->bhkv', k_tail, vf)
        return new_state, inter + intra

    state0 = jnp.zeros((B, H, dk, dv), jnp.float32)
    _, out = lax.scan(step, state0, (qc, kc, vc, cum))
    return jnp.moveaxis(out, 0, 1).reshape(B, S, H, dv)


def attn_mixer(x, w_in, sink, w_out):
    B, S, _ = x.shape
    q, k, v, gate = _split(x @ w_in, A_SPLITS)
    o = window_attention(q.reshape(B, S, A_HEADS, A_HEAD_DIM),
                         k.reshape(B, S, A_KV_HEADS, A_HEAD_DIM),
                         v.reshape(B, S, A_KV_HEADS, A_HEAD_DIM), sink)
    return (o * jax.nn.silu(gate)) @ w_out


def gla_mixer(x, w_in, w_gate_f, b_gate_f, w_gate_b, b_gate_b, head_norm, w_out):
    B, S, _ = x.shape
    q, k, v, gate, lr_f, lr_b = _split(x @ w_in, G_SPLITS)
    q = q.reshape(B, S, G_HEADS, G_KEY_DIM) * G_KEY_DIM ** -0.5
    k = k.reshape(B, S, G_HEADS, G_KEY_DIM)
    v = v.reshape(B, S, G_HEADS, G_VAL_DIM)
    log_f = (jax.nn.log_sigmoid((lr_f @ w_gate_f + b_gate_f).astype(jnp.float32)) / G_GATE_TAU
             ).reshape(B, S, G_HEADS, G_KEY_DIM)
    log_b = (jax.nn.log_sigmoid((lr_b @ w_gate_b + b_gate_b).astype(jnp.float32)) / G_GATE_TAU
             ).reshape(B, S, G_HEADS, G_KEY_DIM)
    o_f = gla_direction(q, k, v, log_f, True)
    flip = lambda t: jnp.flip(t, axis=1)
    o_b = flip(gla_direction(flip(q), flip(k), flip(v), flip(log_b), False))
    of = o_f + o_b
    of = of * lax.rsqrt(jnp.mean(jnp.square(of), axis=-1, keepdims=True) + RMS_EPS)
    o = (of.reshape(B, S, G_VAL_WIDTH) * head_norm.astype(jnp.float32)).astype(x.dtype)
    return (o * jax.nn.silu(gate)) @ w_out


def setup_inputs(seed: int = 0) -> dict:
    key = jax.random.key(seed)
    ks = jax.random.split(key, 20)
    n = lambda k, shape, scale: jax.random.normal(k, shape, jnp.float32) * scale
    D = D_MODEL
    return {
        "x_prompt": n(ks[0], (BATCH, SEQ, D), 1.0),
        "x_sample": n(ks[1], (DEC_BATCH, DEC_SEQ, D), 1.0),
        "l0_w_in": n(ks[2], (D, sum(A_SPLITS)), D ** -0.5),
        "l0_sink": n(ks[3], (A_HEADS,), 0.5),
        "l0_w_out": n(ks[4], (A_WIDTH, D), A_WIDTH ** -0.5 * DN_BETA),
        "l0_ln_g": 1.0 + n(ks[5], (D,), 0.02),
        "l0_ln_b": n(ks[6], (D,), 0.02),
        "l1_w_in": n(ks[7], (D, sum(G_SPLITS)), D ** -0.5),
        "l1_w_gate_f": n(ks[8], (G_GATE_RANK, G_KEY_WIDTH), G_GATE_RANK ** -0.5),
        "l1_b_gate_f": n(ks[9], (G_KEY_WIDTH,), 0.1),
        "l1_w_gate_b": n(ks[10], (G_GATE_RANK, G_KEY_WIDTH), G_GATE_RANK ** -0.5),
        "l1_b_gate_b": n(ks[11], (G_KEY_WIDTH,), 0.1),
        "l1_head_norm": 1.0 + n(ks[12], (G_VAL_WIDTH,), 0.02),
        "l1_w_out": n(ks[13], (G_VAL_WIDTH, D), G_VAL_WIDTH ** -0.5 * DN_BETA),
        "l1_ln_g": 1.0 + n(ks[14], (D,), 0.02),
        "l1_ln_b": n(ks[15], (D,), 0.02),
    }


def reference(x_prompt, x_sample, l0_w_in, l0_sink, l0_w_out, l0_ln_g, l0_ln_b,
              l1_w_in, l1_w_gate_f, l1_b_gate_f, l1_w_gate_b, l1_b_gate_b, l1_head_norm,
              l1_w_out, l1_ln_g, l1_ln_b):
    mixers = (attn_mixer, gla_mixer)
    layer_params = (
        ((l0_w_in, l0_sink, l0_w_out), l0_ln_g, l0_ln_b),
        ((l1_w_in, l1_w_gate_f, l1_b_gate_f, l1_w_gate_b, l1_b_gate_b, l1_head_norm, l1_w_out),
         l1_ln_g, l1_ln_b),
    )

    def trunk(x):
        for i in range(DEPTH):
            p, g, b = layer_params[i]
            x = layer_norm(DN_ALPHA * x + mixers[i % len(mixers)](x, *p), g, b)
        return x

    y_prompt = trunk(x_prompt)
    y_sample = trunk(x_sample)
    return (y_prompt, y_sample)
```

```python
import numpy as np
import ml_dtypes
from contextlib import ExitStack
import concourse.bass as bass
import concourse.mybir as mybir
from concourse.bass_utils import run_bass_kernel_spmd

F32 = mybir.dt.float32
BF16 = mybir.dt.bfloat16
AF = mybir.ActivationFunctionType
ALU = mybir.AluOpType

D = 1024
SEQ = 4096
NCORES = 8
NSEQ = 3
NB = SEQ // 128
NSB = SEQ // 512
DN_ALPHA = float((2 * 2) ** 0.25)
LN_EPS = 1e-5
RMS_EPS = 1e-6
QSCALE1 = float(128 ** -0.5)
SCHED_DEBUG = False


class Buf:
    __slots__ = ("w", "r", "const")

    def __init__(self, const=False):
        self.w = None
        self.r = []
        self.const = const


def bufs(n):
    return [Buf() for _ in range(n)]


class _Rec:
    def __init__(self):
        self.calls = []

    def __getattr__(self, name):
        def call(*a, **k):
            self.calls.append((name, a, k))
            return None
        return call


def _free(ap):
    n = 1
    for d in ap.shape[1:]:
        n *= int(d)
    return n


def _cost(e, calls):
    t = 0.0
    for name, a, k in calls:
        if e == "pe":
            if name == "transpose":
                t += 0.065
            else:
                rhs = k["rhs"]
                n = _free(rhs)
                c = max(0.045, 0.03 + n / 2200.0 * (4.0 if rhs.dtype == F32 else 1.0))
                if int(rhs.shape[0]) <= 64 and n >= 256:
                    c *= 0.62
                t += c
        else:
            out = k.get("out", a[0] if a else None)
            f = _free(out) if out is not None else 64
            if name == "tensor_tensor_scan":
                t += 0.1 + f / 420.0
            elif e == "act":
                t += 0.12 + f / 1150.0
                if name == "activation" and not isinstance(k.get("scale", 1.0), (int, float)):
                    t += 0.2
                if name == "activation" and k.get("func") == AF.Silu:
                    t += 1.0
            elif e == "dve":
                t += 0.08 + f / 900.0
            else:
                t += 0.1 + f / 480.0
    return t


class _Op:
    __slots__ = ("id", "e", "calls", "deps", "est", "dma", "ev", "prev_ev", "ndep", "users", "ready", "fin", "bl")


class Sched:
    WINDOW = 128
    LAT_X = 0.25
    LAT_S = 0.12

    def __init__(self, nc, es, ndma=32, reorder=True):
        self.nc = nc
        self.E = {"pe": nc.tensor, "act": nc.scalar, "dve": nc.vector, "pool": nc.gpsimd, "sp": nc.sync}
        self.sem = {k: es.enter_context(nc.semaphore("s_" + k)) for k in ("pe", "act", "dve", "pool")}
        self.cnt = {k: 0 for k in self.sem}
        self.dsem = [es.enter_context(nc.semaphore("d%d" % i)) for i in range(ndma)]
        self.dcnt = [0] * ndma
        self.drr = 0
        self.waited = {k: {} for k in self.E}
        self.ops = []
        self.base = 0
        self.reorder = reorder

    def _semof(self, k):
        return self.sem[k] if isinstance(k, str) else self.dsem[k]

    def _wait(self, e, evs):
        need = {}
        for ev in evs:
            if ev is None:
                continue
            k, v = ev
            if k == e and e == "pe":
                continue
            if self.waited[e].get(k, 0) >= v:
                continue
            if need.get(k, 0) < v:
                need[k] = v
        for k, v in need.items():
            self.E[e].wait_ge(self._semof(k), v)
            self.waited[e][k] = v

    def _record(self, e, calls, reads, writes, dma=None):
        op = _Op()
        op.id = self.base + len(self.ops)
        op.e = e
        op.calls = calls
        op.dma = dma
        deps = set()
        for b in reads:
            if b.w is not None:
                deps.add(b.w)
        for b in writes:
            if b.w is not None:
                deps.add(b.w)
            deps.update(b.r)
        op.deps = set(d for d in deps if d >= self.base)
        op.est = _cost(e, calls) if dma is None else 0.06
        self.ops.append(op)
        for b in reads:
            if not b.const:
                b.r.append(op.id)
        for b in writes:
            b.w = op.id
            b.r = []
        return op

    def op(self, e, fn, reads=(), writes=()):
        rec = _Rec()
        fn(rec)
        self._record(e, rec.calls, reads, writes)

    def dma(self, q, out, in_, reads=(), writes=()):
        nbytes = 1
        for d in out.shape:
            nbytes *= int(d)
        nbytes *= 2 if out.dtype == BF16 else 4
        self._record("sp", [("dma_start", (), {"out": out, "in_": in_})], reads, writes, dma=2.0 + nbytes / 200e3)

    def _schedule(self):
        ops = self.ops
        order = {k: [] for k in self.E}
        if not self.reorder:
            for op in ops:
                order[op.e].append(op)
            return order
        LAT_X, LAT_S = self.LAT_X, self.LAT_S
        base = self.base
        pend = {k: [] for k in self.E}
        for op in ops:
            op.ndep = len(op.deps)
            op.users = []
            op.ready = 0.0
            pend[op.e].append(op)
        for op in ops:
            for d in op.deps:
                ops[d - base].users.append(op)
        for op in reversed(ops):
            bl = 0.0
            for u in op.users:
                v = u.bl + (LAT_X if u.e != op.e else LAT_S)
                if v > bl:
                    bl = v
            op.bl = bl + (op.est if op.dma is None else op.dma)
        free = {k: 0.0 for k in self.E}
        head = {k: 0 for k in self.E}
        done = [False] * len(ops)
        remaining = len(ops)
        W = self.WINDOW
        while remaining:
            best = None
            for e, lst in pend.items():
                h = head[e]
                while h < len(lst) and done[lst[h].id - base]:
                    h += 1
                head[e] = h
                lim = min(len(lst), h + W)
                fe = free[e]
                cand = None
                early = None
                for i in range(h, lim):
                    op = lst[i]
                    if done[op.id - base] or op.ndep:
                        continue
                    if op.ready <= fe:
                        if cand is None or op.bl > cand.bl:
                            cand = op
                    elif cand is None and (early is None or op.ready < early.ready):
                        early = op
                pick = cand if cand is not None else early
                if pick is None:
                    continue
                st = pick.ready if pick.ready > fe else fe
                key = (st, pick.id)
                if best is None or key < best[0]:
                    best = (key, pick)
            assert best is not None, "scheduler stuck (dependency cycle?)"
            (st, _), op = best
            e = op.e
            if op.dma is not None:
                free[e] = st + op.est
                op.fin = st + op.dma
            else:
                op.fin = st + op.est
                free[e] = op.fin
            done[op.id - base] = True
            remaining -= 1
            order[e].append(op)
            for u in op.users:
                u.ndep -= 1
                t = op.fin + (LAT_X if u.e != e else (0.0 if e == "pe" else LAT_S))
                if t > u.ready:
                    u.ready = t
        if SCHED_DEBUG:
            busy = {k: 0.0 for k in self.E}
            for op in ops:
                busy[op.e] += op.est
            print("[sched] window ops=%d makespan=%.1f us busy=%s" % (len(ops), max(op.fin for op in ops),
                  {k: round(v, 1) for k, v in busy.items()}), flush=True)
        return order

    def flush(self):
        if not self.ops:
            return
        order = self._schedule()
        for e in ("pe", "act", "dve", "pool"):
            for i, op in enumerate(order[e]):
                op.ev = (e, self.cnt[e] + i + 1)
        for op in order["sp"]:
            i = self.drr
            self.drr = (i + 1) % len(self.dsem)
            op.prev_ev = (i, self.dcnt[i]) if self.dcnt[i] > 0 else None
            self.dcnt[i] += 16
            op.ev = (i, self.dcnt[i])
        ops = self.ops
        for e in ("sp", "pe", "act", "dve", "pool"):
            eng = self.E[e]
            for op in order[e]:
                evs = [ops[d - self.base].ev for d in op.deps]
                if e == "sp":
                    evs.append(op.prev_ev)
                self._wait(e, evs)
                ins = None
                for name, a, k in op.calls:
                    ins = getattr(eng, name)(*a, **k)
                if e == "sp":
                    ins.then_inc(self.dsem[op.ev[0]], 16)
                else:
                    ins.then_inc(self.sem[e], 1)
            if e != "sp":
                self.cnt[e] += len(order[e])
        self.base += len(self.ops)
        self.ops = []

    def barrier(self):
        self.flush()
        evs = [(k, c) for k, c in self.cnt.items() if c > 0] + [(i, c) for i, c in enumerate(self.dcnt) if c > 0]
        for e in self.E:
            self._wait(e, evs)


class Banks:
    def __init__(self, nc, es):
        self.t = [es.enter_context(nc.psum_tensor("psb%d" % i, [128, 512], F32)) for i in range(8)]
        self.b = bufs(8)
        self.roles = {}
        self.ptr = {}

    def config(self, roles):
        self.roles = roles
        self.ptr = {k: 0 for k in roles}

    def next(self, role="mm"):
        ids = self.roles[role]
        i = ids[self.ptr[role] % len(ids)]
        self.ptr[role] += 1
        b = self.b[i]
        assert b.w is None or b.r, "PSUM bank %d (%s) re-allocated before its content was read" % (i, role)
        return self.t[i], b


def copy_op(S, eng, out, in_, reads, writes):
    if eng == "act":
        S.op("act", lambda e: e.copy(out=out, in_=in_), reads, writes)
    else:
        S.op(eng, lambda e: e.tensor_copy(out=out, in_=in_), reads, writes)


def layer_norm_rows(S, z, zb, lng, lnb, Bc, tmp, tmpb, norm_eng="act", mhalf=None, junk=None, junkb=None):
    st, mv, ve = tmp
    if junk is not None:
        S.op("act", lambda e: e.activation(out=z[:, :], in_=z[:, :], func=AF.Identity, accum_out=st[:, 0, 0:1]), [zb], [zb, tmpb])
        S.op("act", lambda e: e.activation(out=junk, in_=z[:, :], func=AF.Square, accum_out=st[:, 0, 1:2]), [zb, tmpb], [junkb, tmpb])
        S.op("dve", lambda e: e.tensor_scalar(out=mv[:, 0:2], in0=st[:, 0, 0:2], scalar1=1.0 / 1024, scalar2=None, op0=ALU.mult),
             [tmpb], [tmpb])
        S.op("dve", lambda e: e.scalar_tensor_tensor(out=ve[:, 1:2], in0=mv[:, 0:1], scalar=-1.0, in1=mv[:, 0:1],
                                                     op0=ALU.mult, op1=ALU.mult), [tmpb], [tmpb])
        S.op("dve", lambda e: e.scalar_tensor_tensor(out=ve[:, 0:1], in0=mv[:, 1:2], scalar=LN_EPS, in1=ve[:, 1:2],
                                                     op0=ALU.add, op1=ALU.add), [tmpb], [tmpb])
    else:
        def f(e):
            i = None
            for c in range(2):
                i = e.bn_stats(out=st[:, c, :], in_=z[:, c * 512:(c + 1) * 512])
            return i
        S.op("dve", f, [zb], [tmpb])
        S.op("dve", lambda e: e.bn_aggr(out=mv[:, :], in_=st[:, :, :].rearrange("p a b -> p (a b)")), [tmpb], [tmpb])
        S.op("dve", lambda e: e.tensor_scalar(out=ve[:, 0:1], in0=mv[:, 1:2], scalar1=LN_EPS, scalar2=None, op0=ALU.add),
             [tmpb], [tmpb])
    if mhalf is None:
        S.op("act", lambda e: e.activation(out=ve[:, 1:2], in_=ve[:, 0:1], func=AF.Ln), [tmpb], [tmpb])
        S.op("act", lambda e: e.activation(out=ve[:, 2:3], in_=ve[:, 1:2], func=AF.Exp, scale=-0.5), [tmpb], [tmpb])
    else:
        S.op("pool", lambda e: e.tensor_tensor(out=ve[:, 2:3], in0=ve[:, 0:1], in1=mhalf[:, 0:1], op=ALU.pow), [tmpb, Bc], [tmpb])
    S.op("dve", lambda e: e.scalar_tensor_tensor(out=ve[:, 3:4], in0=mv[:, 0:1], scalar=-1.0, in1=ve[:, 2:3],
                                                 op0=ALU.mult, op1=ALU.mult), [tmpb], [tmpb])
    S.op("act", lambda e: e.activation(out=z[:, :], in_=z[:, :], func=AF.Identity, scale=ve[:, 2:3], bias=ve[:, 3:4]),
         [zb, tmpb], [zb])
    S.op("dve", lambda e: e.tensor_tensor(out=z[:, :], in0=z[:, :], in1=lng[:, :], op=ALU.mult), [zb, Bc], [zb])
    S.op("dve", lambda e: e.tensor_tensor(out=z[:, :], in0=z[:, :], in1=lnb[:, :], op=ALU.add), [zb, Bc], [zb])


def transpose_rows(S, PS, identb, Bc, src, srcb, dst_fn, dstb, evac=("act", "dve"), role="mm"):
    pt, pb = PS.next(role)
    ptb = pt[:, :].bitcast(BF16)

    def f(e):
        i = None
        for kc in range(8):
            i = e.transpose(out=ptb[:, kc * 128:(kc + 1) * 128], in_=src[:, kc * 128:(kc + 1) * 128], identity=identb[:, :])
        return i
    S.op("pe", f, [srcb, Bc], [pb])
    for half in range(2):
        copy_op(S, evac[half % len(evac)], dst_fn(half * 4),
                ptb[:, half * 512:(half + 1) * 512].rearrange("p (a b) -> p a b", a=4), [pb], [dstb])


def proj_group(S, PS, out_cols, lhsT_fn, rhs_fn, reads, role="mm", rows=128):
    pt, pb = PS.next(role)

    def f(e):
        i = None
        for kc in range(8):
            i = e.matmul(pt[0:rows, 0:out_cols], lhsT=lhsT_fn(kc), rhs=rhs_fn(kc), start=(kc == 0), stop=(kc == 7))
        return i
    S.op("pe", f, reads, [pb])
    return pt, pb


def build(nseq=NSEQ, debug=False, phases=("A1", "A2", "B")):
    nc = bass.Bass("TRN2", target_bir_lowering=False)
    NT = nseq * SEQ

    def dram(name, shape, dt, kind):
        return nc.dram_tensor(name, shape, dt, kind=kind).ap()
    IN = "ExternalInput"
    SCR = "ExternalOutput" if debug else "Internal"
    x = dram("x", [NT, D], F32, IN)
    y = dram("y", [NT, D], F32, "ExternalOutput")
    W = {}
    for name, shape in (("l0_w_in", [D, 2560]), ("l0_sink", [16]), ("l0_w_out", [D, D]), ("l0_ln_g", [D]), ("l0_ln_b", [D]),
                        ("l1_w_in", [D, 3104]), ("l1_w_gate_f", [16, 512]), ("l1_b_gate_f", [128, 4]),
                        ("l1_w_gate_b", [16, 512]), ("l1_b_gate_b", [128, 4]), ("l1_head_norm", [D]),
                        ("l1_w_out", [D, D]), ("l1_ln_g", [D]), ("l1_ln_b", [D]),
                        ("c_ident", [128, 128]), ("c_etab", [128, 3 * 4 * 512]), ("c_scanmask", [128, 512]),
                        ("c_trif", [128, 128]), ("c_trib", [128, 128])):
        W[name] = dram(name, shape, F32, IN)
    x1s = dram("x1s", [NT, D], F32, SCR)
    ofs = dram("ofs", [NT, 2 * D], BF16, SCR)
    sgs = dram("sgs", [NT, D], F32, SCR)
    vs = dram("vs", [NT, D], BF16, SCR)
    qdbs = dram("qdbs", [nseq * NSB, 128, 4 * 512], BF16, SCR)
    kibs = dram("kibs", [nseq * NSB, 128, 4 * 512], BF16, SCR)
    ktbs = dram("ktbs", [NT, 512], BF16, SCR)
    ecls = dram("ecls", [nseq * NSB, 128, 32], F32, SCR)

    with ExitStack() as es:
        S = Sched(nc, es)
        PS = Banks(nc, es)
        if "A1" in phases:
            phase_a1(nc, S, PS, W, x, x1s, nseq)
            S.barrier()
        if "A2" in phases:
            phase_a2(nc, S, PS, W, x1s, ofs, sgs, vs, qdbs, kibs, ktbs, ecls, nseq)
            S.barrier()
        if "B" in phases:
            phase_b(nc, S, PS, W, x1s, ofs, sgs, vs, qdbs, kibs, ktbs, ecls, y, nseq)
        S.barrier()
    return nc


def run_pipeline(units, nst):
    n = len(units)
    for t in range(n + nst - 1):
        for si in range(nst - 1, -1, -1):
            u = t - si
            if 0 <= u < n:
                units[u][si]()


def phase_a1(nc, S, PS, W, x, x1s, nseq):
    PS.config({"sc": [0, 1, 2, 3], "o": [4, 5], "mm": [6, 7]})
    with ExitStack() as ph:
        def T(n, sh, dt):
            return ph.enter_context(nc.sbuf_tensor(n, sh, dt))
        w0q = T("w0q", [128, 8, 1024], BF16)
        w0k = T("w0k", [128, 8, 4, 128], BF16)
        w0v = T("w0v", [128, 8, 256], BF16)
        w0g = T("w0g", [128, 8, 1024], BF16)
        w0o = T("w0o", [128, 8, 1024], BF16)
        etab = T("etab", [128, 3, 4, 512], F32)
        ident = T("ident", [128, 128], F32)
        identb = T("identb", [128, 128], BF16)
        lng = T("lng0", [128, 1024], F32)
        lnb = T("lnb0", [128, 1024], F32)
        esk = T("esk", [128, 16], F32)
        mhalf = T("mhalf", [128, 4], F32)
        Bc = Buf(const=True)
        S.op("pool", lambda e: e.memset(mhalf[:, :], -0.5), [], [Bc])
        S.dma("sp", etab[:, :, :, :].rearrange("p a b c -> p (a b c)"), W["c_etab"][:, :], writes=[Bc])
        S.dma("sp", ident[:, :], W["c_ident"][:, :], writes=[Bc])
        S.op("dve", lambda e: e.tensor_copy(out=identb[:, :], in_=ident[:, :]), [Bc], [Bc])
        S.dma("sp", lng[:, :], W["l0_ln_g"].partition_broadcast(128), writes=[Bc])
        S.dma("sp", lnb[:, :], W["l0_ln_b"].partition_broadcast(128), writes=[Bc])
        S.dma("sp", esk[:, :], W["l0_sink"].partition_broadcast(128), writes=[Bc])
        S.op("act", lambda e: e.activation(out=esk[:, :], in_=esk[:, :], func=AF.Exp), [Bc], [Bc])
        S.op("dve", lambda e: e.tensor_scalar(out=esk[:, :], in0=esk[:, :], scalar1=2.0, scalar2=None, op0=ALU.mult), [Bc], [Bc])
        with ExitStack() as st:
            NS_ = 4
            stg = [st.enter_context(nc.sbuf_tensor("stg%d" % i, [128, 2560], F32)) for i in range(NS_)]
            Bs = bufs(NS_)
            for kc in range(8):
                i = kc % NS_
                S.dma("sp", stg[i][:, :], W["l0_w_in"][kc * 128:(kc + 1) * 128, :], writes=[Bs[i]])
                copy_op(S, "dve", w0q[:, kc, :], stg[i][:, 0:1024], [Bs[i]], [Buf()])
                kv = stg[i][:, 1024:1280].rearrange("p (g d) -> p g d", g=4)
                copy_op(S, "pool", w0k[:, kc, :, 0:64], kv, [Bs[i]], [Buf()])
                copy_op(S, "pool", w0k[:, kc, :, 64:128], kv, [Bs[i]], [Buf()])
                copy_op(S, "pool", w0v[:, kc, :], stg[i][:, 1280:1536], [Bs[i]], [Buf()])
                copy_op(S, "act", w0g[:, kc, :], stg[i][:, 1536:2560], [Bs[i]], [Buf()])
            for kc in range(8):
                i = kc % NS_
                S.dma("sp", stg[i][:, 0:1024], W["l0_w_out"][kc * 128:(kc + 1) * 128, :], writes=[Bs[i]])
                copy_op(S, ("dve", "act")[kc % 2], w0o[:, kc, :], stg[i][:, 0:1024], [Bs[i]], [Buf()])
            S.barrier()
        xa = [T("xa%d" % i, [128, 1024], F32) for i in range(2)]
        xab = bufs(2)
        xr = [T("xr%d" % i, [128, 1024], F32) for i in range(2)]
        xrb = bufs(2)
        xh = [T("xh%d" % i, [128, 1024], BF16) for i in range(2)]
        xhb = bufs(2)
        onh = [T("onh%d" % i, [128, 1024], BF16) for i in range(2)]
        onhb = bufs(2)
        x0T = [T("x0T%d" % i, [128, 8, 512], BF16) for i in range(2)]
        x0Tb = bufs(2)
        qT = [T("qT%d" % i, [128, 8, 512], BF16) for i in range(2)]
        qTb = bufs(2)
        kT = [T("kT%d" % i, [128, 4, 512], BF16) for i in range(3)]
        kTb = bufs(3)
        va = [T("va%d" % i, [128, 4, 4, 80], BF16) for i in range(3)]
        vab = bufs(3)
        sg = [T("sg%d" % i, [128, 1024], F32) for i in range(2)]
        sgb = bufs(2)
        ex = [T("ex%d" % i, [128, 512], F32) for i in range(2)]
        exb = bufs(2)
        pT = [T("pT%d" % i, [128, 3, 512], BF16) for i in range(2)]
        pTb = [[bufs(2) for _ in range(3)] for _ in range(2)]
        den = [T("den%d" % i, [128, 16], F32) for i in range(2)]
        rden = [T("rden%d" % i, [128, 16], F32) for i in range(2)]
        denb = [bufs(4) for _ in range(2)]
        on = [T("on%d" % i, [128, 1024], F32) for i in range(2)]
        onb = bufs(2)
        onT = [T("onT%d" % i, [128, 8, 128], BF16) for i in range(2)]
        onTb = bufs(2)
        z = [T("z%d" % i, [128, 1024], F32) for i in range(2)]
        zb = bufs(2)
        lnt = [(T("lnst%d" % i, [128, 2, 6], F32), T("lnmv%d" % i, [128, 2], F32), T("lnve%d" % i, [128, 4], F32)) for i in range(2)]
        lntb = bufs(2)
        for i in range(3):
            S.op("dve", lambda e: e.memset(va[i][:, :, :, :], 1.0), [], [vab[i]])
        cnt = {"ex": 0, "blk": 0}
        sbs = [(s, k) for s in range(nseq) for k in range(NSB)]

        def xT_blocks(gi, blocks):
            s, k = sbs[gi]
            slot = gi % 2
            for b in blocks:
                row0 = s * SEQ + k * 512 + b * 128
                a = cnt["blk"] % 2
                cnt["blk"] += 1
                S.dma("sp", xa[a][:, :], x[row0:row0 + 128, :], writes=[xab[a]])
                S.op("act", lambda e: e.copy(out=xh[a][:, :], in_=xa[a][:, :]), [xab[a]], [xhb[a]])
                transpose_rows(S, PS, identb, Bc, xh[a], xhb[a],
                               lambda kc0: x0T[slot][:, kc0:kc0 + 4, b * 128:(b + 1) * 128], x0Tb[slot], evac=("act", "act"))

        def k_proj(gi):
            slot, r3 = gi % 2, gi % 3
            for g in range(4):
                pt, pb = proj_group(S, PS, 512, lambda kc: w0k[:, kc, g, :], lambda kc: x0T[slot][:, kc, :], [Bc, x0Tb[slot]])
                copy_op(S, "act", kT[r3][:, g, :], pt[:, :], [pb], [kTb[r3]])

        def v_proj(gi):
            slot, r3 = gi % 2, gi % 3
            for b in range(4):
                pt, pb = proj_group(S, PS, 256, lambda kc: x0T[slot][:, kc, b * 128:(b + 1) * 128], lambda kc: w0v[:, kc, :],
                                    [Bc, x0Tb[slot]])
                copy_op(S, "act", va[r3][:, b, :, 0:64], pt[:, 0:256].rearrange("p (g d) -> p g d", g=4), [pb], [vab[r3]])

        def q_proj(gi, chunks):
            slot = gi % 2
            for c in chunks:
                pt, pb = proj_group(S, PS, 512, lambda kc: w0q[:, kc, c * 128:(c + 1) * 128], lambda kc: x0T[slot][:, kc, :],
                                    [Bc, x0Tb[slot]])
                copy_op(S, "act", qT[slot][:, c, :], pt[:, :], [pb], [qTb[slot]])

        def sb_piece(gi, i):
            if gi >= len(sbs):
                return
            if i == 0:
                xT_blocks(gi, (0, 1))
            elif i == 1:
                xT_blocks(gi, (2, 3))
                k_proj(gi)
            elif i == 2:
                v_proj(gi)
                q_proj(gi, range(0, 4))
            else:
                q_proj(gi, range(4, 8))

        def gate(gi, b):
            if gi >= len(sbs):
                return
            slot = gi % 2
            a = (4 * gi + b) % 2
            for half in range(2):
                pt, pb = proj_group(S, PS, 512, lambda kc: x0T[slot][:, kc, b * 128:(b + 1) * 128],
                                    lambda kc: w0g[:, kc, half * 512:(half + 1) * 512], [Bc, x0Tb[slot]])
                S.op("act", lambda e: e.activation(out=sg[a][:, half * 512:(half + 1) * 512], in_=pt[:, :], func=AF.Tanh, scale=0.5),
                     [pb], [sgb[a]])
                S.op("dve", lambda e: e.scalar_tensor_tensor(out=sg[a][:, half * 512:(half + 1) * 512],
                                                             in0=sg[a][:, half * 512:(half + 1) * 512], scalar=1.0, in1=pt[:, :],
                                                             op0=ALU.add, op1=ALU.mult), [pb, sgb[a]], [sgb[a]])

        def scores(gi, n, b, g, js):
            pp = g % 2
            slot = gi % 2
            groups = [js[0:2]] + ([js[2:3]] if len(js) > 2 else [])
            for grp in groups:
                L = len(grp)
                bk = [PS.next("sc") for _ in range(2)]
                rd = [qTb[slot]]

                def f(e):
                    i = None
                    for idx, j in enumerate(grp):
                        nj = n + j - 1
                        gj = gi + (nj // 4 - n // 4)
                        bb = nj % 4
                        for hf in range(2):
                            i = e.matmul(bk[hf][0][:, idx * 256:(idx + 1) * 256].rearrange("p (a b) -> p a b", a=2),
                                         lhsT=kT[gj % 3][hf * 64:(hf + 1) * 64, g, bb * 128:(bb + 1) * 128],
                                         rhs=qT[slot][hf * 64:(hf + 1) * 64, 2 * g:2 * g + 2, b * 128:(b + 1) * 128],
                                         start=True, stop=True)
                    return i
                for j in grp:
                    rd.append(kTb[(gi + ((n + j - 1) // 4 - n // 4)) % 3])
                S.op("pe", f, rd, [bk[0][1], bk[1][1]])
                j0 = grp[0]
                for hf in range(2):
                    pt, pb = bk[hf]
                    xi = cnt["ex"] % 2
                    cnt["ex"] += 1
                    S.op("act", lambda e: e.activation(out=ex[xi][:, 0:L * 256], in_=pt[:, 0:L * 256], func=AF.Exp, scale=0.125),
                         [pb], [exb[xi]])
                    S.op("dve", lambda e: e.tensor_tensor(
                        out=pT[pp][:, j0:j0 + L, hf * 256:(hf + 1) * 256],
                        in0=ex[xi][:, 0:L * 256].rearrange("p (l c) -> p l c", l=L),
                        in1=etab[:, j0:j0 + L, g, hf * 256:(hf + 1) * 256], op=ALU.mult),
                        [exb[xi], Bc], [pTb[pp][j][hf] for j in grp])

        def pv(gi, n, g, js, a):
            pp = g % 2
            ot, ob = PS.next("o")

            def f(e):
                i = None
                for r in range(4):
                    off = (r % 2) * 256 + (r // 2) * 128
                    for ji, j in enumerate(js):
                        nj = n + j - 1
                        gj = gi + (nj // 4 - n // 4)
                        i = e.matmul(ot[:, r * 128:r * 128 + 65], lhsT=pT[pp][:, j, off:off + 128],
                                     rhs=va[gj % 3][:, nj % 4, g, 0:65], start=(ji == 0), stop=(ji == len(js) - 1))
                return i
            rd = [pTb[pp][j][hf_] for j in js for hf_ in range(2)] + [vab[(gi + ((n + j - 1) // 4 - n // 4)) % 3] for j in js]
            S.op("pe", f, rd, [ob])
            obv = ot[:, :].rearrange("p (r d) -> p r d", r=4)
            dn, rdn, dnb = den[a], rden[a], denb[a][g]
            S.op("dve", lambda e: e.scalar_tensor_tensor(out=dn[:, 4 * g:4 * g + 4], in0=obv[:, :, 64], scalar=2.0,
                                                         in1=esk[:, 4 * g:4 * g + 4], op0=ALU.mult, op1=ALU.add), [ob, Bc], [dnb])
            S.op("dve", lambda e: e.reciprocal(out=rdn[:, 4 * g:4 * g + 4], in_=dn[:, 4 * g:4 * g + 4]), [dnb], [dnb])
            S.op("dve", lambda e: e.tensor_tensor(
                out=on[a][:, g * 256:(g + 1) * 256].rearrange("p (r d) -> p r d", r=4), in0=obv[:, :, 0:64],
                in1=rdn[:, 4 * g:4 * g + 4].unsqueeze(2).to_broadcast([128, 4, 64]), op=ALU.mult), [ob, dnb], [onb[a]])

        def make_unit(gi, b):
            s, k = sbs[gi]
            n = 4 * k + b
            row0 = s * SEQ + n * 128
            js = [j for j in range(3) if 0 <= n + j - 1 < NB]
            a = (4 * gi + b) % 2

            def s0():
                if gi == 0 and b == 0:
                    for i in range(4):
                        sb_piece(0, i)
                    gate(0, 0)
                sb_piece(gi + 1, b)
                if b < 3:
                    gate(gi, b + 1)
                else:
                    gate(gi + 1, 0)
                scores(gi, n, b, 0, js)
                for g in range(4):
                    if g + 1 < 4:
                        scores(gi, n, b, g + 1, js)
                    pv(gi, n, g, js, a)
                S.op("dve", lambda e: e.tensor_tensor(out=onh[a][:, :], in0=on[a][:, :], in1=sg[a][:, :], op=ALU.mult),
                     [onb[a], sgb[a]], [onhb[a]])

            def s1():
                transpose_rows(S, PS, identb, Bc, onh[a], onhb[a], lambda kc0: onT[a][:, kc0:kc0 + 4, :], onTb[a], evac=("act", "act"))
                S.dma("sp", xr[a][:, :], x[row0:row0 + 128, :], writes=[xrb[a]])
                for half in range(2):
                    pt, pb = proj_group(S, PS, 512, lambda kc: onT[a][:, kc, :], lambda kc: w0o[:, kc, half * 512:(half + 1) * 512],
                                        [onTb[a], Bc])
                    S.op("dve", lambda e: e.scalar_tensor_tensor(out=z[a][:, half * 512:(half + 1) * 512],
                                                                 in0=xr[a][:, half * 512:(half + 1) * 512], scalar=DN_ALPHA,
                                                                 in1=pt[:, :], op0=ALU.mult, op1=ALU.add), [pb, xrb[a]], [zb[a]])

            def s2():
                layer_norm_rows(S, z[a], zb[a], lng, lnb, Bc, lnt[a], lntb[a], mhalf=mhalf)
                S.dma("sp", x1s[row0:row0 + 128, :], z[a][:, :], reads=[zb[a]])
            return [s0, s1, s2]

        units = [make_unit(gi, b) for gi in range(len(sbs)) for b in range(4)]
        run_pipeline(units, 3)


def gla_block(S, PS, G, b, chunk_order, cnt):
    at, ab = PS.next("at")

    def f(e):
        i = None
        for h in range(4):
            i = e.matmul(at[:, h * 128:(h + 1) * 128], lhsT=G["ki"][:, h, b * 128:(b + 1) * 128],
                         rhs=G["qdu"][:, h, b * 128:(b + 1) * 128], start=True, stop=True)
        return i
    S.op("pe", f, [G["kib"], G["qdub"]], [ab])
    attm, attmb = G["attm"], G["attmb"]
    S.op("dve", lambda e: e.tensor_tensor(out=attm[:, :, :], in0=at[:, :].rearrange("p (h t) -> p h t", h=4),
                                          in1=G["trim"][:, :].unsqueeze(1).to_broadcast([128, 4, 128]), op=ALU.mult),
         [ab, G["Bc"]], [attmb])
    ob = [PS.next("o") for _ in range(2)]
    obufs = [x_[1] for x_ in ob]
    zr = G["zeros"]

    def f(e):
        i = None
        for i2 in range(2):
            e.matmul(ob[i2][0][:, :], lhsT=zr[0:1, 0:128], rhs=zr[0:1, 0:512], start=True, stop=False)
        for h in range(4):
            i = e.matmul(ob[h // 2][0][:, (h % 2) * 256:(h % 2 + 1) * 256], lhsT=attm[:, h, :],
                         rhs=G["v"][:, b, h * 256:(h + 1) * 256], start=False, stop=False)
        if "add" in G:
            add = G["add"]
            for i2 in range(2):
                for part in range(2):
                    i = e.matmul(ob[i2][0][:, :], lhsT=G["identb"][:, :], rhs=add[:, part, i2 * 512:(i2 + 1) * 512],
                                 start=False, stop=False)
        return i
    S.op("pe", f, [attmb, G["vb"], G["Bc"]] + ([G["addb"]] if "add" in G else []), obufs)
    St, Stb, Sbf, Sbfb = G["St"], G["Stb"], G["Sbf"], G["Sbfb"]
    for ci, cl in enumerate(chunk_order):
        c = 2 * b + cl
        sp = cnt["s"] % 2
        last = ci == 1

        def f(e):
            i = None
            for h in range(4):
                i = e.matmul(ob[h // 2][0][:, (h % 2) * 256:(h % 2 + 1) * 256], lhsT=G["qdp"][:, h, c, :], rhs=Sbf[sp][:, h, :],
                             start=False, stop=(last and h % 2 == 1))
            return i
        S.op("pe", f, [G["qdpb"], Sbfb[sp]], obufs)
        ub = [PS.next("u") for _ in range(2)]

        def f(e):
            i = None
            for h in range(4):
                i = e.matmul(ub[h // 2][0][:, (h % 2) * 256:(h % 2 + 1) * 256],
                             lhsT=G["kt"][cl * 64:(cl + 1) * 64, b, h * 128:(h + 1) * 128],
                             rhs=G["v"][cl * 64:(cl + 1) * 64, b, h * 256:(h + 1) * 256], start=True, stop=True)
            return i
        S.op("pe", f, [G["ktb"], G["vb"]], [ub[0][1], ub[1][1]])
        for h in range(4):
            u, ubb = ub[h // 2]
            S.op("dve", lambda e: e.scalar_tensor_tensor(out=St[:, h, :], in0=St[:, h, :], scalar=G["ecl"][:, h, c:c + 1],
                                                         in1=u[:, (h % 2) * 256:(h % 2 + 1) * 256], op0=ALU.mult, op1=ALU.add),
                 [Stb[h], ubb, G["eclb"]], [Stb[h]])
        cnt["s"] += 1
        sn = cnt["s"] % 2
        S.op("act", lambda e: e.copy(out=Sbf[sn][:, :, :], in_=St[:, :, :]), Stb, [Sbfb[sn]])
    return ob


def phase_a2(nc, S, PS, W, x1s, ofs, sgs, vs, qdbs, kibs, ktbs, ecls, nseq):
    PS.config({"at": [0], "o": [1, 2], "u": [3, 4], "mm": [5, 6, 7]})
    with ExitStack() as ph:
        def T(n, sh, dt):
            return ph.enter_context(nc.sbuf_tensor(n, sh, dt))
        w1q = T("w1q", [128, 8, 512], BF16)
        w1k = T("w1k", [128, 8, 512], BF16)
        w1lr = T("w1lr", [128, 8, 64], BF16)
        w1v = T("w1v", [128, 8, 1024], BF16)
        w1g = T("w1g", [128, 8, 1024], BF16)
        wgt = T("wgt", [64, 512], BF16)
        nbias = T("nbias", [128, 2, 4], F32)
        hn = T("hn", [128, 1024], F32)
        ident = T("ident2", [128, 128], F32)
        identb = T("identb2", [128, 128], BF16)
        smask = T("smask", [128, 512], F32)
        trif = T("trif", [128, 128], F32)
        zeros = T("zeros2", [1, 512], BF16)
        Bc = Buf(const=True)
        S.dma("sp", ident[:, :], W["c_ident"][:, :], writes=[Bc])
        S.op("dve", lambda e: e.tensor_copy(out=identb[:, :], in_=ident[:, :]), [Bc], [Bc])
        S.dma("sp", smask[:, :], W["c_scanmask"][:, :], writes=[Bc])
        S.dma("sp", trif[:, :], W["c_trif"][:, :], writes=[Bc])
        S.dma("sp", hn[:, :], W["l1_head_norm"].partition_broadcast(128), writes=[Bc])
        S.dma("sp", nbias[:, 0, :], W["l1_b_gate_f"][:, :], writes=[Bc])
        S.dma("sp", nbias[:, 1, :], W["l1_b_gate_b"][:, :], writes=[Bc])
        S.op("dve", lambda e: e.tensor_scalar(out=nbias[:, :, :], in0=nbias[:, :, :], scalar1=-1.0, scalar2=None, op0=ALU.mult),
             [Bc], [Bc])
        S.op("dve", lambda e: e.memset(w1lr[:, :, :], 0.0), [], [Bc])
        S.op("dve", lambda e: e.memset(wgt[:, :], 0.0), [], [Bc])
        S.op("dve", lambda e: e.memset(zeros[:, :], 0.0), [], [Bc])
        with ExitStack() as st:
            NS_ = 4
            stg = [st.enter_context(nc.sbuf_tensor("stgb%d" % i, [128, 3104], F32)) for i in range(NS_)]
            Bs = bufs(NS_)
            for kc in range(8):
                i = kc % NS_
                S.dma("sp", stg[i][:, :], W["l1_w_in"][kc * 128:(kc + 1) * 128, :], writes=[Bs[i]])
                copy_op(S, "dve", w1q[:, kc, :], stg[i][:, 0:512], [Bs[i]], [Buf()])
                copy_op(S, "pool", w1k[:, kc, :], stg[i][:, 512:1024], [Bs[i]], [Buf()])
                copy_op(S, "act", w1v[:, kc, :], stg[i][:, 1024:2048], [Bs[i]], [Buf()])
                copy_op(S, "dve", w1g[:, kc, :], stg[i][:, 2048:3072], [Bs[i]], [Buf()])
                copy_op(S, "pool", w1lr[:, kc, 0:16], stg[i][:, 3072:3088], [Bs[i], Bc], [Buf()])
                copy_op(S, "pool", w1lr[:, kc, 32:48], stg[i][:, 3088:3104], [Bs[i], Bc], [Buf()])
            gst = st.enter_context(nc.sbuf_tensor("stgg", [64, 512], F32))
            Bg = Buf()
            S.dma("sp", gst[0:16, :], W["l1_w_gate_f"][:, :], writes=[Bg])
            S.dma("sp", gst[32:48, :], W["l1_w_gate_b"][:, :], writes=[Bg])
            copy_op(S, "dve", wgt[0:16, :], gst[0:16, :], [Bg, Bc], [Buf()])
            copy_op(S, "dve", wgt[32:48, :], gst[32:48, :], [Bg, Bc], [Buf()])
            S.barrier()
        xl = [T("xl%d" % i, [128, 1024], F32) for i in range(2)]
        xlb = bufs(2)
        xlh = [T("xlh%d" % i, [128, 1024], BF16) for i in range(2)]
        xlhb = bufs(2)
        x1T_ = [T("x1T%d" % i, [128, 8, 512], BF16) for i in range(2)]
        x1Tb_ = bufs(2)
        lrT = T("lrT", [64, 512], BF16)
        lrTb = Buf()
        v1 = [T("v1_%d" % i, [128, 4, 1024], BF16) for i in range(2)]
        v1b = bufs(2)
        sg1 = [T("sg1_%d" % i, [128, 1024], F32) for i in range(2)]
        sg1b = bufs(2)
        qf_ = [T("qf%d" % i, [128, 512], F32) for i in range(2)]
        kf_ = [T("kf%d" % i, [128, 512], F32) for i in range(2)]
        qfb_, kfb_ = bufs(2), bufs(2)
        Pt = [T("Pt%d" % i, [128, 520], F32) for i in range(2)]
        Ptb = bufs(2)
        Pd = [T("Pd%d" % i, [128, 512], F32) for i in range(2)]
        Pdb = bufs(2)
        tp = [T("tp%d" % i, [128, 8], F32) for i in range(2)]
        eb = [T("eb%d" % i, [128, 512], F32) for i in range(2)]
        ebb = bufs(2)
        enb = [T("enb%d" % i, [128, 512], F32) for i in range(2)]
        enbb = bufs(2)
        eclf = [T("eclf%d" % i, [128, 4, 8], F32) for i in range(2)]
        eclfb = bufs(2)
        qduf = [T("qduf%d" % i, [128, 4, 512], BF16) for i in range(2)]
        qdufb = bufs(2)
        qdp = [T("qdp%d" % i, [128, 4, 8, 128], BF16) for i in range(2)]
        qdpb = bufs(2)
        kif = [T("kif%d" % i, [128, 4, 512], BF16) for i in range(2)]
        kifb = bufs(2)
        ktf = [T("ktf%d" % i, [128, 4, 512], BF16) for i in range(2)]
        ktfb = bufs(2)
        eclB = T("eclB", [128, 4, 8], F32)
        eclBb = Buf()
        qduB = T("qduB", [128, 4, 512], BF16)
        qduBb = Buf()
        kiB = T("kiB", [128, 4, 512], BF16)
        kiBb = Buf()
        ktB = T("ktB", [128, 4, 512], BF16)
        ktBb = Buf()
        ktT_ = [T("ktT%d" % i, [128, 512], BF16) for i in range(2)]
        ktTb_ = bufs(2)
        attm = T("attm", [128, 4, 128], BF16)
        attmb = Buf()
        St = T("St", [128, 4, 256], F32)
        Stb = bufs(4)
        Sbf = [T("Sbf%d" % i, [128, 4, 256], BF16) for i in range(2)]
        Sbfb = bufs(2)
        of = [T("of%d" % i, [128, 2, 1024], BF16) for i in range(2)]
        ofb = bufs(2)
        for i in range(2):
            S.op("pool", lambda e: e.memset(qdp[i][:, :, :, :], 0.0), [], [qdpb[i]])
            S.op("pool", lambda e: e.memset(Pt[i][:, :], 0.0), [], [Ptb[i]])
        cnt = {"s": 0, "blk": 0, "g": 0}

        def prep_pieces(s, k):
            sb = s * NSB + k
            tok0 = s * SEQ + k * 512
            p = k % 2
            x1T, x1Tb = x1T_[p], x1Tb_[p]

            def piece0():
                for b in range(4):
                    a = cnt["blk"] % 2
                    cnt["blk"] += 1
                    S.dma("sp", xl[a][:, :], x1s[tok0 + b * 128:tok0 + (b + 1) * 128, :], writes=[xlb[a]])
                    S.op("act", lambda e: e.copy(out=xlh[a][:, :], in_=xl[a][:, :]), [xlb[a]], [xlhb[a]])
                    transpose_rows(S, PS, identb, Bc, xlh[a], xlhb[a], lambda kc0: x1T[:, kc0:kc0 + 4, b * 128:(b + 1) * 128], x1Tb,
                                   evac=("act", "act"))
                pt, pb = proj_group(S, PS, 512, lambda kc: w1lr[:, kc, :], lambda kc: x1T[:, kc, :], [Bc, x1Tb], rows=64)
                copy_op(S, "act", lrT[:, :], pt[0:64, :], [pb], [lrTb])
                for b in range(4):
                    for half in range(2):
                        pt, pb = proj_group(S, PS, 512, lambda kc: x1T[:, kc, b * 128:(b + 1) * 128],
                                            lambda kc: w1v[:, kc, half * 512:(half + 1) * 512], [Bc, x1Tb])
                        copy_op(S, "act", v1[p][:, b, half * 512:(half + 1) * 512], pt[:, :], [pb], [v1b[p]])
                S.dma("sp", vs[tok0:tok0 + 512, :].rearrange("(b p) c -> p b c", p=128), v1[p][:, :, :], reads=[v1b[p]])

            def piece_gate():
                for b in range(4):
                    a = cnt["g"] % 2
                    cnt["g"] += 1
                    for half in range(2):
                        pt, pb = proj_group(S, PS, 512, lambda kc: x1T[:, kc, b * 128:(b + 1) * 128],
                                            lambda kc: w1g[:, kc, half * 512:(half + 1) * 512], [Bc, x1Tb])
                        S.op("act", lambda e: e.activation(out=sg1[a][:, half * 512:(half + 1) * 512], in_=pt[:, :], func=AF.Silu),
                             [pb], [sg1b[a]])
                    S.op("dve", lambda e: e.tensor_tensor(out=sg1[a][:, :], in0=sg1[a][:, :], in1=hn[:, :], op=ALU.mult),
                         [sg1b[a], Bc], [sg1b[a]])
                    S.dma("sp", sgs[tok0 + b * 128:tok0 + (b + 1) * 128, :], sg1[a][:, :], reads=[sg1b[a]])

            def piece_head(h):
                qf, kf, qfb, kfb = qf_[h % 2], kf_[h % 2], qfb_[h % 2], kfb_[h % 2]

                def fn():
                    for (wt_, dst, dstb) in ((w1q, qf, qfb), (w1k, kf, kfb)):
                        pt, pb = proj_group(S, PS, 512, lambda kc: wt_[:, kc, h * 128:(h + 1) * 128], lambda kc: x1T[:, kc, :],
                                            [Bc, x1Tb])
                        copy_op(S, "act", dst[:, :], pt[:, :], [pb], [dstb])
                    for d in range(2):
                        ktT, ktTb = ktT_[d], ktTb_[d]
                        rows = slice(0, 16) if d == 0 else slice(32, 48)
                        ecl_, eclb_ = (eclf[p], eclfb[p]) if d == 0 else (eclB, eclBb)
                        qdu_, qdub_ = (qduf[p], qdufb[p]) if d == 0 else (qduB, qduBb)
                        ki_, kib_ = (kif[p], kifb[p]) if d == 0 else (kiB, kiBb)
                        kt_, ktb_ = (ktf[p], ktfb[p]) if d == 0 else (ktB, ktBb)
                        pt, pb = PS.next("mm")
                        S.op("pe", lambda e: e.matmul(pt[:, :], lhsT=wgt[rows, h * 128:(h + 1) * 128], rhs=lrT[rows, :],
                                                      start=True, stop=True), [Bc, lrTb], [pb])
                        P_ = Pt[d]
                        S.op("act", lambda e: e.activation(out=P_[:, 1:513], in_=pt[:, :], func=AF.Exp, scale=-1.0,
                                                           bias=nbias[:, d, h:h + 1]), [pb, Bc], [Ptb[d]])
                        S.op("act", lambda e: e.activation(out=P_[:, 1:513], in_=P_[:, 1:513], func=AF.Ln, bias=1.0),
                             [Ptb[d]], [Ptb[d]])
                        Pv = Pd[d][:, :].rearrange("p (c t) -> p c t", c=8)
                        if d == 0:
                            S.op("dve", lambda e: e.tensor_tensor_scan(out=Pd[d][:, :], data0=smask[:, :], data1=P_[:, 1:513],
                                                                       initial=0.0, op0=ALU.mult, op1=ALU.add),
                                 [Ptb[d], Bc], [Pdb[d]])
                            plast = Pv[:, :, 63]
                        else:
                            S.op("dve", lambda e: e.tensor_tensor_scan(out=Pd[d][:, :], data0=P_[:, 0:512], data1=smask[:, :],
                                                                       initial=0.0, op0=ALU.add, op1=ALU.mult),
                                 [Ptb[d], Bc], [Pdb[d]])
                            S.op("dve", lambda e: e.tensor_tensor(
                                out=tp[d][:, :], in0=Pv[:, :, 63],
                                in1=P_[:, 1:513].rearrange("p (c t) -> p c t", c=8)[:, :, 63], op=ALU.add),
                                [Pdb[d], Ptb[d]], [Pdb[d]])
                            S.op("dve", lambda e: e.tensor_tensor(
                                out=Pv, in0=tp[d][:, :].unsqueeze(2).to_broadcast([128, 8, 64]), in1=Pv, op=ALU.subtract),
                                [Pdb[d]], [Pdb[d]])
                            plast = tp[d][:, :]
                        S.op("act", lambda e: e.activation(out=ecl_[:, h, :], in_=plast, func=AF.Exp, scale=-1.0 / 16),
                             [Pdb[d]], [eclb_])
                        S.op("act", lambda e: e.activation(out=eb[d][:, :], in_=Pd[d][:, :], func=AF.Exp, scale=-1.0 / 16),
                             [Pdb[d]], [ebb[d]])
                        S.op("act", lambda e: e.activation(out=enb[d][:, :], in_=Pd[d][:, :], func=AF.Exp, scale=1.0 / 16),
                             [Pdb[d]], [enbb[d]])
                        S.op("dve", lambda e: e.scalar_tensor_tensor(out=qdu_[:, h, :], in0=qf[:, :], scalar=QSCALE1, in1=eb[d][:, :],
                                                                     op0=ALU.mult, op1=ALU.mult), [qfb, ebb[d]], [qdub_])
                        S.op("dve", lambda e: e.tensor_tensor(out=ki_[:, h, :], in0=kf[:, :], in1=enb[d][:, :], op=ALU.mult),
                             [kfb, enbb[d]], [kib_])
                        env = enb[d][:, :].rearrange("p (c t) -> p c t", c=8)
                        S.op("dve", lambda e: e.tensor_tensor(
                            out=env, in0=env, in1=ecl_[:, h, :].unsqueeze(2).to_broadcast([128, 8, 64]), op=ALU.mult),
                            [enbb[d], eclb_], [enbb[d]])
                        S.op("dve", lambda e: e.tensor_tensor(out=ktT[:, :], in0=kf[:, :], in1=enb[d][:, :], op=ALU.mult),
                             [kfb, enbb[d]], [ktTb])
                        pt2, pb2 = PS.next("mm")
                        pt2b = pt2[:, :].bitcast(BF16)

                        def f(e):
                            i = None
                            for b in range(4):
                                i = e.transpose(out=pt2b[:, b * 128:(b + 1) * 128], in_=ktT[:, b * 128:(b + 1) * 128], identity=identb[:, :])
                            return i
                        S.op("pe", f, [ktTb, Bc], [pb2])
                        copy_op(S, "act", kt_[:, :, h * 128:(h + 1) * 128], pt2b[:, 0:512].rearrange("p (b k) -> p b k", b=4),
                                [pb2], [ktb_])
                    for par in range(2):
                        S.op("act", lambda e: e.copy(
                            out=qdp[p][:, h, par::2, par * 64:(par + 1) * 64],
                            in_=qduf[p][:, h, :].rearrange("p (b c t) -> p b c t", b=4, c=2)[:, :, par, :]), [qdufb[p]], [qdpb[p]])
                    if h == 3:
                        S.dma("sp", qdbs[sb, :, :], qduB[:, :, :].rearrange("p h t -> p (h t)"), reads=[qduBb])
                        S.dma("sp", kibs[sb, :, :], kiB[:, :, :].rearrange("p h t -> p (h t)"), reads=[kiBb])
                        S.dma("sp", ktbs[tok0:tok0 + 512, :].rearrange("(b p) c -> p b c", p=128), ktB[:, :, :], reads=[ktBb])
                        S.dma("sp", ecls[sb, :, :], eclB[:, :, :].rearrange("p h c -> p (h c)"), reads=[eclBb])
                return fn
            return [piece0, piece_gate, piece_head(0), piece_head(1), piece_head(2), piece_head(3)]

        def gla_pieces(s, k):
            tok0 = s * SEQ + k * 512
            p = k % 2
            G = dict(ki=kif[p], kib=kifb[p], qdu=qduf[p], qdub=qdufb[p], qdp=qdp[p], qdpb=qdpb[p], kt=ktf[p], ktb=ktfb[p],
                     v=v1[p], vb=v1b[p], ecl=eclf[p], eclb=eclfb[p], St=St, Stb=Stb, Sbf=Sbf, Sbfb=Sbfb, attm=attm, attmb=attmb,
                     trim=trif, Bc=Bc, zeros=zeros)

            def blk(b):
                def fn():
                    if k == 0 and b == 0:
                        S.op("dve", lambda e: e.memset(St[:, :, :], 0.0), [], Stb)
                        S.op("dve", lambda e: e.memset(Sbf[cnt["s"] % 2][:, :, :], 0.0), [], [Sbfb[cnt["s"] % 2]])
                    ob = gla_block(S, PS, G, b, (0, 1), cnt)
                    a = (4 * k + b) % 2
                    for i2 in range(2):
                        sl = slice(i2 * 512, (i2 + 1) * 512)
                        S.op("act", lambda e: e.copy(out=of[a][:, 0, sl], in_=ob[i2][0][:, :]), [ob[i2][1]], [ofb[a]])
                        S.op("dve", lambda e: e.tensor_tensor(out=of[a][:, 1, sl], in0=ob[i2][0][:, :], in1=of[a][:, 0, sl],
                                                              op=ALU.subtract), [ob[i2][1], ofb[a]], [ofb[a]])
                    S.dma("sp", ofs[tok0 + b * 128:tok0 + (b + 1) * 128, :], of[a][:, :, :].rearrange("p a c -> p (a c)"),
                          reads=[ofb[a]])
                return fn
            return [blk(0), blk(1), blk(2), blk(3)]

        sbs = [(s, k) for s in range(nseq) for k in range(NSB)]
        for fn in prep_pieces(*sbs[0]):
            fn()
        for i, (s, k) in enumerate(sbs):
            gp = gla_pieces(s, k)
            pp_ = prep_pieces(*sbs[i + 1]) if i + 1 < len(sbs) else []
            order = [gp[0]] + pp_[0:1] + [gp[1]] + pp_[1:3] + [gp[2]] + pp_[3:5] + [gp[3]] + pp_[5:6]
            for fn in order:
                fn()


def phase_b(nc, S, PS, W, x1s, ofs, sgs, vs, qdbs, kibs, ktbs, ecls, y, nseq):
    PS.config({"at": [0], "o": [1, 2, 3, 4], "u": [5, 6], "mm": [7, 0]})
    with ExitStack() as ph:
        def T(n, sh, dt):
            return ph.enter_context(nc.sbuf_tensor(n, sh, dt))
        w1o = T("w1o", [128, 8, 1024], BF16)
        lng = T("lng1", [128, 1024], F32)
        lnb = T("lnb1", [128, 1024], F32)
        ident = T("ident3", [128, 128], F32)
        identb = T("identb3", [128, 128], BF16)
        trib = T("trib", [128, 128], F32)
        zeros = T("zeros3", [1, 512], BF16)
        Bc = Buf(const=True)
        S.dma("sp", ident[:, :], W["c_ident"][:, :], writes=[Bc])
        S.op("dve", lambda e: e.tensor_copy(out=identb[:, :], in_=ident[:, :]), [Bc], [Bc])
        S.dma("sp", trib[:, :], W["c_trib"][:, :], writes=[Bc])
        S.dma("sp", lng[:, :], W["l1_ln_g"].partition_broadcast(128), writes=[Bc])
        S.dma("sp", lnb[:, :], W["l1_ln_b"].partition_broadcast(128), writes=[Bc])
        S.op("dve", lambda e: e.memset(zeros[:, :], 0.0), [], [Bc])
        with ExitStack() as st:
            NS_ = 4
            stg = [st.enter_context(nc.sbuf_tensor("stgc%d" % i, [128, 1024], F32)) for i in range(NS_)]
            Bs = bufs(NS_)
            for kc in range(8):
                i = kc % NS_
                S.dma("sp", stg[i][:, :], W["l1_w_out"][kc * 128:(kc + 1) * 128, :], writes=[Bs[i]])
                copy_op(S, ("dve", "act")[kc % 2], w1o[:, kc, :], stg[i][:, :], [Bs[i]], [Buf()])
            S.barrier()
        qdu = [T("bqdu%d" % i, [128, 4, 512], BF16) for i in range(2)]
        qdub = bufs(2)
        qdp = [T("bqdp%d" % i, [128, 4, 8, 128], BF16) for i in range(2)]
        qdpb = bufs(2)
        ki = [T("bki%d" % i, [128, 4, 512], BF16) for i in range(2)]
        kib_ = bufs(2)
        kt = [T("bkt%d" % i, [128, 4, 512], BF16) for i in range(2)]
        ktb_ = bufs(2)
        v = [T("bv%d" % i, [128, 4, 1024], BF16) for i in range(2)]
        vb_ = bufs(2)
        ecl = [T("becl%d" % i, [128, 4, 8], F32) for i in range(2)]
        eclb_ = bufs(2)
        R3 = 3
        sgh = [T("bsg%d" % i, [128, 1024], F32) for i in range(R3)]
        sghb = bufs(R3)
        xl = [T("bx%d" % i, [128, 1024], F32) for i in range(R3)]
        xlb = bufs(R3)
        ofl = [T("bof%d" % i, [128, 2, 1024], BF16) for i in range(R3)]
        oflb = bufs(R3)
        attm = T("battm", [128, 4, 128], BF16)
        attmb = Buf()
        St = T("bSt", [128, 4, 256], F32)
        Stb = bufs(4)
        Sbf = [T("bSbf%d" % i, [128, 4, 256], BF16) for i in range(2)]
        Sbfb = bufs(2)
        osum_ = [T("osum%d" % i, [128, 1024], F32) for i in range(2)]
        osumb_ = bufs(2)
        junk_ = [T("junk%d" % i, [128, 4, 256], F32) for i in range(2)]
        ss_ = [T("ss%d" % i, [128, 8], F32) for i in range(2)]
        ssb_ = bufs(2)
        on_ = [T("bon%d" % i, [128, 1024], BF16) for i in range(2)]
        onb_ = bufs(2)
        onT_ = [T("bonT%d" % i, [128, 8, 128], BF16) for i in range(2)]
        onTb_ = bufs(2)
        z = [T("bz%d" % i, [128, 1024], F32) for i in range(2)]
        zb = bufs(2)
        lnt = [(T("blnst%d" % i, [128, 2, 6], F32), T("blnmv%d" % i, [128, 2], F32), T("blnve%d" % i, [128, 4], F32)) for i in range(2)]
        lntb = bufs(2)
        for i in range(2):
            S.op("pool", lambda e: e.memset(qdp[i][:, :, :, :], 0.0), [], [qdpb[i]])
        cnt = {"s": 0}
        obs = {}

        def make_unit(ui, s, k, b, p):
            sb = s * NSB + k
            tok0 = s * SEQ + k * 512
            r0 = tok0 + b * 128
            a3 = ui % R3
            a2 = ui % 2
            osum, osumb, junk, ss, ssb = osum_[a2], osumb_[a2], junk_[a2], ss_[a2], ssb_[a2]
            on, onb, onT, onTb = on_[a2], onb_[a2], onT_[a2], onTb_[a2]
            G = dict(ki=ki[p], kib=kib_[p], qdu=qdu[p], qdub=qdub[p], qdp=qdp[p], qdpb=qdpb[p], kt=kt[p], ktb=ktb_[p],
                     v=v[p], vb=vb_[p], ecl=ecl[p], eclb=eclb_[p], St=St, Stb=Stb, Sbf=Sbf, Sbfb=Sbfb, attm=attm, attmb=attmb,
                     trim=trib, Bc=Bc, zeros=zeros)

            def s0():
                if b == 3:
                    S.dma("sp", qdu[p][:, :, :].rearrange("p h t -> p (h t)"), qdbs[sb, :, :], writes=[qdub[p]])
                    S.dma("sp", ki[p][:, :, :].rearrange("p h t -> p (h t)"), kibs[sb, :, :], writes=[kib_[p]])
                    S.dma("sp", kt[p][:, :, :], ktbs[tok0:tok0 + 512, :].rearrange("(b p) c -> p b c", p=128), writes=[ktb_[p]])
                    S.dma("sp", v[p][:, :, :], vs[tok0:tok0 + 512, :].rearrange("(b p) c -> p b c", p=128), writes=[vb_[p]])
                    S.dma("sp", ecl[p][:, :, :].rearrange("p h c -> p (h c)"), ecls[sb, :, :], writes=[eclb_[p]])
                    for h in range(4):
                        for par in range(2):
                            S.op("act", lambda e: e.copy(
                                out=qdp[p][:, h, par::2, par * 64:(par + 1) * 64],
                                in_=qdu[p][:, h, :].rearrange("p (b c t) -> p b c t", b=4, c=2)[:, :, par, :]), [qdub[p]], [qdpb[p]])
                    if k == NSB - 1:
                        S.op("dve", lambda e: e.memset(St[:, :, :], 0.0), [], Stb)
                        S.op("dve", lambda e: e.memset(Sbf[cnt["s"] % 2][:, :, :], 0.0), [], [Sbfb[cnt["s"] % 2]])
                S.dma("sp", sgh[a3][:, :], sgs[r0:r0 + 128, :], writes=[sghb[a3]])
                S.dma("sp", xl[a3][:, :], x1s[r0:r0 + 128, :], writes=[xlb[a3]])
                S.dma("sp", ofl[a3][:, :, :].rearrange("p a c -> p (a c)"), ofs[r0:r0 + 128, :], writes=[oflb[a3]])
                G2 = dict(G)
                G2.update(add=ofl[a3], addb=oflb[a3], identb=identb)
                obs[ui] = gla_block(S, PS, G2, b, (1, 0), cnt)

            def s1():
                ob = obs.pop(ui)
                obufs = [ob[0][1], ob[1][1]]

                def ov(h):
                    return ob[h // 2][0][:, (h % 2) * 256:(h % 2 + 1) * 256]

                def f(e):
                    i = None
                    for h in range(4):
                        i = e.activation(out=junk[:, h, :], in_=ov(h), func=AF.Square, accum_out=ss[:, h:h + 1])
                    return i
                S.op("act", f, obufs, [ssb])
                S.op("dve", lambda e: e.tensor_scalar(out=ss[:, 0:4], in0=ss[:, 0:4], scalar1=1.0 / 256, scalar2=RMS_EPS,
                                                      op0=ALU.mult, op1=ALU.add), [ssb], [ssb])
                S.op("act", lambda e: e.activation(out=ss[:, 0:4], in_=ss[:, 0:4], func=AF.Ln), [ssb], [ssb])
                S.op("act", lambda e: e.activation(out=ss[:, 4:8], in_=ss[:, 0:4], func=AF.Exp, scale=-0.5), [ssb], [ssb])
                for h in range(4):
                    S.op("dve", lambda e: e.scalar_tensor_tensor(
                        out=on[:, h * 256:(h + 1) * 256], in0=ov(h), scalar=ss[:, 4 + h:5 + h],
                        in1=sgh[a3][:, h * 256:(h + 1) * 256], op0=ALU.mult, op1=ALU.mult),
                        [obufs[h // 2], ssb, sghb[a3]], [onb])
                transpose_rows(S, PS, identb, Bc, on, onb, lambda kc0: onT[:, kc0:kc0 + 4, :], onTb, evac=("act", "dve"))
                for half in range(2):
                    pt, pb = proj_group(S, PS, 512, lambda kc: onT[:, kc, :], lambda kc: w1o[:, kc, half * 512:(half + 1) * 512],
                                        [onTb, Bc])
                    S.op("dve", lambda e: e.scalar_tensor_tensor(out=z[a2][:, half * 512:(half + 1) * 512],
                                                                 in0=xl[a3][:, half * 512:(half + 1) * 512], scalar=DN_ALPHA,
                                                                 in1=pt[:, :], op0=ALU.mult, op1=ALU.add), [pb, xlb[a3]], [zb[a2]])

            def s2():
                layer_norm_rows(S, z[a2], zb[a2], lng, lnb, Bc, lnt[a2], lntb[a2],
                                junk=junk[:, :, :].rearrange("p a b -> p (a b)"), junkb=ssb)
                S.dma("sp", y[r0:r0 + 128, :], z[a2][:, :], reads=[zb[a2]])
            return [s0, s1, s2]

        units = []
        ui = 0
        sbi = 0
        for s in range(nseq):
            for k in range(NSB - 1, -1, -1):
                p = sbi % 2
                sbi += 1
                for b in range(3, -1, -1):
                    units.append(make_unit(ui, s, k, b, p))
                    ui += 1
        run_pipeline(units, 3)


def host_consts():
    H = 16
    slopes = (2.0 ** (-8.0 * np.arange(1, H + 1) / H)).astype(np.float64)
    kl = np.arange(128)[:, None].astype(np.float64)
    ql = np.arange(128)[None, :].astype(np.float64)
    etab = np.zeros((128, 3, 4, 4, 128), np.float64)
    order = [0, 2, 1, 3]
    for j in range(3):
        if j == 0:
            dist = 128 + ql - kl
            ok = kl >= ql
        elif j == 1:
            dist = np.abs(ql - kl)
            ok = np.ones((128, 128), bool)
        else:
            dist = 128 + kl - ql
            ok = kl <= ql
        for g in range(4):
            for pos, r in enumerate(order):
                h = 4 * g + r
                etab[:, j, g, pos, :] = np.where(ok, np.exp(-slopes[h] * dist), 0.0)
    t = np.arange(512)
    scanmask = np.broadcast_to((t % 64 != 0).astype(np.float32)[None, :], (128, 512)).copy()
    s_ = np.arange(128)[:, None]
    t_ = np.arange(128)[None, :]
    same = (s_ // 64) == (t_ // 64)
    trif = (same & (t_ >= s_)).astype(np.float32)
    trib = (same & (t_ < s_)).astype(np.float32)
    return {
        "c_ident": np.eye(128, dtype=np.float32),
        "c_etab": etab.reshape(128, -1).astype(np.float32),
        "c_scanmask": scanmask,
        "c_trif": trif,
        "c_trib": trib,
    }


_NC_CACHE = {}


def kernel(x_prompt, x_sample, l0_w_in, l0_sink, l0_w_out, l0_ln_g, l0_ln_b,
           l1_w_in, l1_w_gate_f, l1_b_gate_f, l1_w_gate_b, l1_b_gate_b, l1_head_norm,
           l1_w_out, l1_ln_g, l1_ln_b):
    f32 = lambda a: np.ascontiguousarray(np.asarray(a, dtype=np.float32))
    xp = f32(x_prompt).reshape(-1, SEQ, D)
    xs = f32(x_sample).reshape(-1, SEQ, D)
    nP = xp.shape[0]
    xall = np.concatenate([xp, xs], axis=0)
    shared = {
        "l0_w_in": f32(l0_w_in), "l0_sink": f32(l0_sink), "l0_w_out": f32(l0_w_out), "l0_ln_g": f32(l0_ln_g), "l0_ln_b": f32(l0_ln_b),
        "l1_w_in": f32(l1_w_in), "l1_w_gate_f": f32(l1_w_gate_f),
        "l1_b_gate_f": f32(np.asarray(l1_b_gate_f).reshape(4, 128).T),
        "l1_w_gate_b": f32(l1_w_gate_b),
        "l1_b_gate_b": f32(np.asarray(l1_b_gate_b).reshape(4, 128).T),
        "l1_head_norm": f32(l1_head_norm), "l1_w_out": f32(l1_w_out), "l1_ln_g": f32(l1_ln_g), "l1_ln_b": f32(l1_ln_b),
    }
    shared.update(host_consts())
    if "nc" not in _NC_CACHE:
        _NC_CACHE["nc"] = build()
    nc = _NC_CACHE["nc"]
    in_maps = []
    for c in range(NCORES):
        m = dict(shared)
        m["x"] = np.ascontiguousarray(xall[c * NSEQ:(c + 1) * NSEQ].reshape(NSEQ * SEQ, D))
        in_maps.append(m)
    res = run_bass_kernel_spmd(nc, in_maps, core_ids=list(range(NCORES)))
    yall = np.stack([np.asarray(r["y"], dtype=np.float32).reshape(NSEQ, SEQ, D) for r in res.results], axis=0)
    yall = yall.reshape(NCORES * NSEQ, SEQ, D)
    return (np.ascontiguousarray(yall[:nP]), np.ascontiguousarray(yall[nP:]))
```

```python
import numpy as np
import ml_dtypes
from contextlib import ExitStack
import concourse.bass as bass
import concourse.mybir as mybir
from concourse.bass_utils import run_bass_kernel_spmd

F32 = mybir.dt.float32
BF16 = mybir.dt.bfloat16
AF = mybir.ActivationFunctionType
ALU = mybir.AluOpType

D = 1024
SEQ = 4096
NCORES = 8
NSEQ = 3
NB = SEQ // 128
NSB = SEQ // 512
DN_ALPHA = float((2 * 2) ** 0.25)
LN_EPS = 1e-5
RMS_EPS = 1e-6
QSCALE1 = float(128 ** -0.5)
SCHED_DEBUG = False


class Buf:
    __slots__ = ("w", "r", "const")

    def __init__(self, const=False):
        self.w = None
        self.r = []
        self.const = const


def bufs(n):
    return [Buf() for _ in range(n)]


class _Rec:
    def __init__(self):
        self.calls = []

    def __getattr__(self, name):
        def call(*a, **k):
            self.calls.append((name, a, k))
            return None
        return call


def _free(ap):
    n = 1
    for d in ap.shape[1:]:
        n *= int(d)
    return n


def _cost(e, calls):
    t = 0.0
    for name, a, k in calls:
        if e == "pe":
            if name == "transpose":
                t += 0.065
            else:
                rhs = k["rhs"]
                n = _free(rhs)
                c = max(0.045, 0.03 + n / 2200.0 * (4.0 if rhs.dtype == F32 else 1.0))
                if int(rhs.shape[0]) <= 64 and n >= 256:
                    c *= 0.62
                t += c
        else:
            out = k.get("out", a[0] if a else None)
            f = _free(out) if out is not None else 64
            if name == "tensor_tensor_scan":
                t += 0.1 + f / 420.0
            elif e == "act":
                t += 0.12 + f / 1150.0
                if name == "activation" and not isinstance(k.get("scale", 1.0), (int, float)):
                    t += 0.2
                if name == "activation" and k.get("func") == AF.Silu:
                    t += 1.0
            elif e == "dve":
                t += 0.08 + f / 900.0
            else:
                t += 0.1 + f / 480.0
    return t


class _Op:
    __slots__ = ("id", "e", "calls", "deps", "est", "dma", "ev", "prev_ev", "ndep", "users", "ready", "fin", "bl")


class Sched:
    WINDOW = 128
    LAT_X = 0.25
    LAT_S = 0.12

    def __init__(self, nc, es, ndma=32, reorder=True):
        self.nc = nc
        self.E = {"pe": nc.tensor, "act": nc.scalar, "dve": nc.vector, "pool": nc.gpsimd, "sp": nc.sync}
        self.sem = {k: es.enter_context(nc.semaphore("s_" + k)) for k in ("pe", "act", "dve", "pool")}
        self.cnt = {k: 0 for k in self.sem}
        self.dsem = [es.enter_context(nc.semaphore("d%d" % i)) for i in range(ndma)]
        self.dcnt = [0] * ndma
        self.drr = 0
        self.waited = {k: {} for k in self.E}
        self.ops = []
        self.base = 0
        self.reorder = reorder

    def _semof(self, k):
        return self.sem[k] if isinstance(k, str) else self.dsem[k]

    def _wait(self, e, evs):
        need = {}
        for ev in evs:
            if ev is None:
                continue
            k, v = ev
            if k == e and e == "pe":
                continue
            if self.waited[e].get(k, 0) >= v:
                continue
            if need.get(k, 0) < v:
                need[k] = v
        for k, v in need.items():
            self.E[e].wait_ge(self._semof(k), v)
            self.waited[e][k] = v

    def _record(self, e, calls, reads, writes, dma=None):
        op = _Op()
        op.id = self.base + len(self.ops)
        op.e = e
        op.calls = calls
        op.dma = dma
        deps = set()
        for b in reads:
            if b.w is not None:
                deps.add(b.w)
        for b in writes:
            if b.w is not None:
                deps.add(b.w)
            deps.update(b.r)
        op.deps = set(d for d in deps if d >= self.base)
        op.est = _cost(e, calls) if dma is None else 0.06
        self.ops.append(op)
        for b in reads:
            if not b.const:
                b.r.append(op.id)
        for b in writes:
            b.w = op.id
            b.r = []
        return op

    def op(self, e, fn, reads=(), writes=()):
        rec = _Rec()
        fn(rec)
        self._record(e, rec.calls, reads, writes)

    def dma(self, q, out, in_, reads=(), writes=()):
        nbytes = 1
        for d in out.shape:
            nbytes *= int(d)
        nbytes *= 2 if out.dtype == BF16 else 4
        self._record("sp", [("dma_start", (), {"out": out, "in_": in_})], reads, writes, dma=2.0 + nbytes / 200e3)

    def _schedule(self):
        ops = self.ops
        order = {k: [] for k in self.E}
        if not self.reorder:
            for op in ops:
                order[op.e].append(op)
            return order
        LAT_X, LAT_S = self.LAT_X, self.LAT_S
        base = self.base
        pend = {k: [] for k in self.E}
        for op in ops:
            op.ndep = len(op.deps)
            op.users = []
            op.ready = 0.0
            pend[op.e].append(op)
        for op in ops:
            for d in op.deps:
                ops[d - base].users.append(op)
        for op in reversed(ops):
            bl = 0.0
            for u in op.users:
                v = u.bl + (LAT_X if u.e != op.e else LAT_S)
                if v > bl:
                    bl = v
            op.bl = bl + (op.est if op.dma is None else op.dma)
        free = {k: 0.0 for k in self.E}
        head = {k: 0 for k in self.E}
        done = [False] * len(ops)
        remaining = len(ops)
        W = self.WINDOW
        while remaining:
            best = None
            for e, lst in pend.items():
                h = head[e]
                while h < len(lst) and done[lst[h].id - base]:
                    h += 1
                head[e] = h
                lim = min(len(lst), h + W)
                fe = free[e]
                cand = None
                early = None
                for i in range(h, lim):
                    op = lst[i]
                    if done[op.id - base] or op.ndep:
                        continue
                    if op.ready <= fe:
                        if cand is None or op.bl > cand.bl:
                            cand = op
                    elif cand is None and (early is None or op.ready < early.ready):
                        early = op
                pick = cand if cand is not None else early
                if pick is None:
                    continue
                st = pick.ready if pick.ready > fe else fe
                key = (st, pick.id)
                if best is None or key < best[0]:
                    best = (key, pick)
            assert best is not None, "scheduler stuck (dependency cycle?)"
            (st, _), op = best
            e = op.e
            if op.dma is not None:
                free[e] = st + op.est
                op.fin = st + op.dma
            else:
                op.fin = st + op.est
                free[e] = op.fin
            done[op.id - base] = True
            remaining -= 1
            order[e].append(op)
            for u in op.users:
                u.ndep -= 1
                t = op.fin + (LAT_X if u.e != e else (0.0 if e == "pe" else LAT_S))
                if t > u.ready:
                    u.ready = t
        if SCHED_DEBUG:
            busy = {k: 0.0 for k in self.E}
            for op in ops:
                busy[op.e] += op.est
            print("[sched] window ops=%d makespan=%.1f us busy=%s" % (len(ops), max(op.fin for op in ops),
                  {k: round(v, 1) for k, v in busy.items()}), flush=True)
        return order

    def flush(self):
        if not self.ops:
            return
        order = self._schedule()
        for e in ("pe", "act", "dve", "pool"):
            for i, op in enumerate(order[e]):
                op.ev = (e, self.cnt[e] + i + 1)
        for op in order["sp"]:
            i = self.drr
            self.drr = (i + 1) % len(self.dsem)
            op.prev_ev = (i, self.dcnt[i]) if self.dcnt[i] > 0 else None
            self.dcnt[i] += 16
            op.ev = (i, self.dcnt[i])
        ops = self.ops
        for e in ("sp", "pe", "act", "dve", "pool"):
            eng = self.E[e]
            for op in order[e]:
                evs = [ops[d - self.base].ev for d in op.deps]
                if e == "sp":
                    evs.append(op.prev_ev)
                self._wait(e, evs)
                ins = None
                for name, a, k in op.calls:
                    ins = getattr(eng, name)(*a, **k)
                if e == "sp":
                    ins.then_inc(self.dsem[op.ev[0]], 16)
                else:
                    ins.then_inc(self.sem[e], 1)
            if e != "sp":
                self.cnt[e] += len(order[e])
        self.base += len(self.ops)
        self.ops = []

    def barrier(self):
        self.flush()
        evs = [(k, c) for k, c in self.cnt.items() if c > 0] + [(i, c) for i, c in enumerate(self.dcnt) if c > 0]
        for e in self.E:
            self._wait(e, evs)


class Banks:
    def __init__(self, nc, es):
        self.t = [es.enter_context(nc.psum_tensor("psb%d" % i, [128, 512], F32)) for i in range(8)]
        self.b = bufs(8)
        self.roles = {}
        self.ptr = {}

    def config(self, roles):
        self.roles = roles
        self.ptr = {k: 0 for k in roles}

    def next(self, role="mm"):
        ids = self.roles[role]
        i = ids[self.ptr[role] % len(ids)]
        self.ptr[role] += 1
        b = self.b[i]
        assert b.w is None or b.r, "PSUM bank %d (%s) re-allocated before its content was read" % (i, role)
        return self.t[i], b


def copy_op(S, eng, out, in_, reads, writes):
    if eng == "act":
        S.op("act", lambda e: e.copy(out=out, in_=in_), reads, writes)
    else:
        S.op(eng, lambda e: e.tensor_copy(out=out, in_=in_), reads, writes)


def layer_norm_rows(S, z, zb, lng, lnb, Bc, tmp, tmpb, norm_eng="act", mhalf=None, junk=None, junkb=None):
    st, mv, ve = tmp
    if junk is not None:
        S.op("act", lambda e: e.activation(out=z[:, :], in_=z[:, :], func=AF.Identity, accum_out=st[:, 0, 0:1]), [zb], [zb, tmpb])
        S.op("act", lambda e: e.activation(out=junk, in_=z[:, :], func=AF.Square, accum_out=st[:, 0, 1:2]), [zb, tmpb], [junkb, tmpb])
        S.op("dve", lambda e: e.tensor_scalar(out=mv[:, 0:2], in0=st[:, 0, 0:2], scalar1=1.0 / 1024, scalar2=None, op0=ALU.mult),
             [tmpb], [tmpb])
        S.op("dve", lambda e: e.scalar_tensor_tensor(out=ve[:, 1:2], in0=mv[:, 0:1], scalar=-1.0, in1=mv[:, 0:1],
                                                     op0=ALU.mult, op1=ALU.mult), [tmpb], [tmpb])
        S.op("dve", lambda e: e.scalar_tensor_tensor(out=ve[:, 0:1], in0=mv[:, 1:2], scalar=LN_EPS, in1=ve[:, 1:2],
                                                     op0=ALU.add, op1=ALU.add), [tmpb], [tmpb])
    else:
        def f(e):
            i = None
            for c in range(2):
                i = e.bn_stats(out=st[:, c, :], in_=z[:, c * 512:(c + 1) * 512])
            return i
        S.op("dve", f, [zb], [tmpb])
        S.op("dve", lambda e: e.bn_aggr(out=mv[:, :], in_=st[:, :, :].rearrange("p a b -> p (a b)")), [tmpb], [tmpb])
        S.op("dve", lambda e: e.tensor_scalar(out=ve[:, 0:1], in0=mv[:, 1:2], scalar1=LN_EPS, scalar2=None, op0=ALU.add),
             [tmpb], [tmpb])
    if mhalf is None:
        S.op("act", lambda e: e.activation(out=ve[:, 1:2], in_=ve[:, 0:1], func=AF.Ln), [tmpb], [tmpb])
        S.op("act", lambda e: e.activation(out=ve[:, 2:3], in_=ve[:, 1:2], func=AF.Exp, scale=-0.5), [tmpb], [tmpb])
    else:
        S.op("pool", lambda e: e.tensor_tensor(out=ve[:, 2:3], in0=ve[:, 0:1], in1=mhalf[:, 0:1], op=ALU.pow), [tmpb, Bc], [tmpb])
    S.op("dve", lambda e: e.scalar_tensor_tensor(out=ve[:, 3:4], in0=mv[:, 0:1], scalar=-1.0, in1=ve[:, 2:3],
                                                 op0=ALU.mult, op1=ALU.mult), [tmpb], [tmpb])
    S.op("act", lambda e: e.activation(out=z[:, :], in_=z[:, :], func=AF.Identity, scale=ve[:, 2:3], bias=ve[:, 3:4]),
         [zb, tmpb], [zb])
    S.op("dve", lambda e: e.tensor_tensor(out=z[:, :], in0=z[:, :], in1=lng[:, :], op=ALU.mult), [zb, Bc], [zb])
    S.op("dve", lambda e: e.tensor_tensor(out=z[:, :], in0=z[:, :], in1=lnb[:, :], op=ALU.add), [zb, Bc], [zb])


def transpose_rows(S, PS, identb, Bc, src, srcb, dst_fn, dstb, evac=("act", "dve"), role="mm"):
    pt, pb = PS.next(role)
    ptb = pt[:, :].bitcast(BF16)

    def f(e):
        i = None
        for kc in range(8):
            i = e.transpose(out=ptb[:, kc * 128:(kc + 1) * 128], in_=src[:, kc * 128:(kc + 1) * 128], identity=identb[:, :])
        return i
    S.op("pe", f, [srcb, Bc], [pb])
    for half in range(2):
        copy_op(S, evac[half % len(evac)], dst_fn(half * 4),
                ptb[:, half * 512:(half + 1) * 512].rearrange("p (a b) -> p a b", a=4), [pb], [dstb])


def proj_group(S, PS, out_cols, lhsT_fn, rhs_fn, reads, role="mm", rows=128):
    pt, pb = PS.next(role)

    def f(e):
        i = None
        for kc in range(8):
            i = e.matmul(pt[0:rows, 0:out_cols], lhsT=lhsT_fn(kc), rhs=rhs_fn(kc), start=(kc == 0), stop=(kc == 7))
        return i
    S.op("pe", f, reads, [pb])
    return pt, pb


def build(nseq=NSEQ, debug=False, phases=("A1", "A2", "B")):
    nc = bass.Bass("TRN2", target_bir_lowering=False)
    NT = nseq * SEQ

    def dram(name, shape, dt, kind):
        return nc.dram_tensor(name, shape, dt, kind=kind).ap()
    IN = "ExternalInput"
    SCR = "ExternalOutput" if debug else "Internal"
    x = dram("x", [NT, D], F32, IN)
    y = dram("y", [NT, D], F32, "ExternalOutput")
    W = {}
    for name, shape in (("l0_w_in", [D, 2560]), ("l0_sink", [16]), ("l0_w_out", [D, D]), ("l0_ln_g", [D]), ("l0_ln_b", [D]),
                        ("l1_w_in", [D, 3104]), ("l1_w_gate_f", [16, 512]), ("l1_b_gate_f", [128, 4]),
                        ("l1_w_gate_b", [16, 512]), ("l1_b_gate_b", [128, 4]), ("l1_head_norm", [D]),
                        ("l1_w_out", [D, D]), ("l1_ln_g", [D]), ("l1_ln_b", [D]),
                        ("c_ident", [128, 128]), ("c_etab", [128, 3 * 4 * 512]), ("c_scanmask", [128, 512]),
                        ("c_trif", [128, 128]), ("c_trib", [128, 128])):
        W[name] = dram(name, shape, F32, IN)
    x1s = dram("x1s", [NT, D], F32, SCR)
    ofs = dram("ofs", [NT, D], F32, SCR)
    sgs = dram("sgs", [NT, D], F32, SCR)
    vs = dram("vs", [NT, D], BF16, SCR)
    qdbs = dram("qdbs", [nseq * NSB, 128, 4 * 512], BF16, SCR)
    kibs = dram("kibs", [nseq * NSB, 128, 4 * 512], BF16, SCR)
    ktbs = dram("ktbs", [NT, 512], BF16, SCR)
    ecls = dram("ecls", [nseq * NSB, 128, 32], F32, SCR)

    with ExitStack() as es:
        S = Sched(nc, es)
        PS = Banks(nc, es)
        if "A1" in phases:
            phase_a1(nc, S, PS, W, x, x1s, nseq)
            S.barrier()
        if "A2" in phases:
            phase_a2(nc, S, PS, W, x1s, ofs, sgs, vs, qdbs, kibs, ktbs, ecls, nseq)
            S.barrier()
        if "B" in phases:
            phase_b(nc, S, PS, W, x1s, ofs, sgs, vs, qdbs, kibs, ktbs, ecls, y, nseq)
        S.barrier()
    return nc


def run_pipeline(units, nst):
    n = len(units)
    for t in range(n + nst - 1):
        for si in range(nst - 1, -1, -1):
            u = t - si
            if 0 <= u < n:
                units[u][si]()


def phase_a1(nc, S, PS, W, x, x1s, nseq):
    PS.config({"sc": [0, 1, 2], "o": [3, 4], "mm": [5, 6, 7]})
    with ExitStack() as ph:
        def T(n, sh, dt):
            return ph.enter_context(nc.sbuf_tensor(n, sh, dt))
        w0q = T("w0q", [128, 8, 1024], BF16)
        w0k = T("w0k", [128, 8, 4, 128], BF16)
        w0v = T("w0v", [128, 8, 256], BF16)
        w0g = T("w0g", [128, 8, 1024], BF16)
        w0o = T("w0o", [128, 8, 1024], BF16)
        etab = T("etab", [128, 3, 4, 512], F32)
        ident = T("ident", [128, 128], F32)
        identb = T("identb", [128, 128], BF16)
        lng = T("lng0", [128, 1024], F32)
        lnb = T("lnb0", [128, 1024], F32)
        esk = T("esk", [128, 16], F32)
        mhalf = T("mhalf", [128, 4], F32)
        Bc = Buf(const=True)
        S.op("pool", lambda e: e.memset(mhalf[:, :], -0.5), [], [Bc])
        S.dma("sp", etab[:, :, :, :].rearrange("p a b c -> p (a b c)"), W["c_etab"][:, :], writes=[Bc])
        S.dma("sp", ident[:, :], W["c_ident"][:, :], writes=[Bc])
        S.op("dve", lambda e: e.tensor_copy(out=identb[:, :], in_=ident[:, :]), [Bc], [Bc])
        S.dma("sp", lng[:, :], W["l0_ln_g"].partition_broadcast(128), writes=[Bc])
        S.dma("sp", lnb[:, :], W["l0_ln_b"].partition_broadcast(128), writes=[Bc])
        S.dma("sp", esk[:, :], W["l0_sink"].partition_broadcast(128), writes=[Bc])
        S.op("act", lambda e: e.activation(out=esk[:, :], in_=esk[:, :], func=AF.Exp), [Bc], [Bc])
        S.op("dve", lambda e: e.tensor_scalar(out=esk[:, :], in0=esk[:, :], scalar1=2.0, scalar2=None, op0=ALU.mult), [Bc], [Bc])
        with ExitStack() as st:
            NS_ = 4
            stg = [st.enter_context(nc.sbuf_tensor("stg%d" % i, [128, 2560], F32)) for i in range(NS_)]
            Bs = bufs(NS_)
            for kc in range(8):
                i = kc % NS_
                S.dma("sp", stg[i][:, :], W["l0_w_in"][kc * 128:(kc + 1) * 128, :], writes=[Bs[i]])
                copy_op(S, "dve", w0q[:, kc, :], stg[i][:, 0:1024], [Bs[i]], [Buf()])
                kv = stg[i][:, 1024:1280].rearrange("p (g d) -> p g d", g=4)
                copy_op(S, "pool", w0k[:, kc, :, 0:64], kv, [Bs[i]], [Buf()])
                copy_op(S, "pool", w0k[:, kc, :, 64:128], kv, [Bs[i]], [Buf()])
                copy_op(S, "pool", w0v[:, kc, :], stg[i][:, 1280:1536], [Bs[i]], [Buf()])
                copy_op(S, "act", w0g[:, kc, :], stg[i][:, 1536:2560], [Bs[i]], [Buf()])
            for kc in range(8):
                i = kc % NS_
                S.dma("sp", stg[i][:, 0:1024], W["l0_w_out"][kc * 128:(kc + 1) * 128, :], writes=[Bs[i]])
                copy_op(S, ("dve", "act")[kc % 2], w0o[:, kc, :], stg[i][:, 0:1024], [Bs[i]], [Buf()])
            S.barrier()
        xa = [T("xa%d" % i, [128, 1024], F32) for i in range(2)]
        xab = bufs(2)
        xr = [T("xr%d" % i, [128, 1024], F32) for i in range(2)]
        xrb = bufs(2)
        xh = [T("xh%d" % i, [128, 1024], BF16) for i in range(2)]
        xhb = bufs(2)
        onh = [T("onh%d" % i, [128, 1024], BF16) for i in range(2)]
        onhb = bufs(2)
        x0T = [T("x0T%d" % i, [128, 8, 512], BF16) for i in range(2)]
        x0Tb = bufs(2)
        qT = [T("qT%d" % i, [128, 8, 512], BF16) for i in range(2)]
        qTb = bufs(2)
        kT = [T("kT%d" % i, [128, 4, 512], BF16) for i in range(3)]
        kTb = bufs(3)
        va = [T("va%d" % i, [128, 4, 4, 80], BF16) for i in range(3)]
        vab = bufs(3)
        sg = [T("sg%d" % i, [128, 1024], F32) for i in range(2)]
        sgb = bufs(2)
        ex = [T("ex%d" % i, [128, 512], F32) for i in range(2)]
        exb = bufs(2)
        pT = [T("pT%d" % i, [128, 3, 512], BF16) for i in range(2)]
        pTb = [[bufs(2) for _ in range(3)] for _ in range(2)]
        den = [T("den%d" % i, [128, 16], F32) for i in range(2)]
        rden = [T("rden%d" % i, [128, 16], F32) for i in range(2)]
        denb = [bufs(4) for _ in range(2)]
        on = [T("on%d" % i, [128, 1024], F32) for i in range(2)]
        onb = bufs(2)
        onT = [T("onT%d" % i, [128, 8, 128], BF16) for i in range(2)]
        onTb = bufs(2)
        z = [T("z%d" % i, [128, 1024], F32) for i in range(2)]
        zb = bufs(2)
        lnt = [(T("lnst%d" % i, [128, 2, 6], F32), T("lnmv%d" % i, [128, 2], F32), T("lnve%d" % i, [128, 4], F32)) for i in range(2)]
        lntb = bufs(2)
        for i in range(3):
            S.op("dve", lambda e: e.memset(va[i][:, :, :, :], 1.0), [], [vab[i]])
        cnt = {"ex": 0, "blk": 0}
        sbs = [(s, k) for s in range(nseq) for k in range(NSB)]

        def xT_blocks(gi, blocks):
            s, k = sbs[gi]
            slot = gi % 2
            for b in blocks:
                row0 = s * SEQ + k * 512 + b * 128
                a = cnt["blk"] % 2
                cnt["blk"] += 1
                S.dma("sp", xa[a][:, :], x[row0:row0 + 128, :], writes=[xab[a]])
                S.op("act", lambda e: e.copy(out=xh[a][:, :], in_=xa[a][:, :]), [xab[a]], [xhb[a]])
                transpose_rows(S, PS, identb, Bc, xh[a], xhb[a],
                               lambda kc0: x0T[slot][:, kc0:kc0 + 4, b * 128:(b + 1) * 128], x0Tb[slot], evac=("act", "act"))

        def k_proj(gi):
            slot, r3 = gi % 2, gi % 3
            for g in range(4):
                pt, pb = proj_group(S, PS, 512, lambda kc: w0k[:, kc, g, :], lambda kc: x0T[slot][:, kc, :], [Bc, x0Tb[slot]])
                copy_op(S, "act", kT[r3][:, g, :], pt[:, :], [pb], [kTb[r3]])

        def v_proj(gi):
            slot, r3 = gi % 2, gi % 3
            for b in range(4):
                pt, pb = proj_group(S, PS, 256, lambda kc: x0T[slot][:, kc, b * 128:(b + 1) * 128], lambda kc: w0v[:, kc, :],
                                    [Bc, x0Tb[slot]])
                copy_op(S, "act", va[r3][:, b, :, 0:64], pt[:, 0:256].rearrange("p (g d) -> p g d", g=4), [pb], [vab[r3]])

        def q_proj(gi, chunks):
            slot = gi % 2
            for c in chunks:
                pt, pb = proj_group(S, PS, 512, lambda kc: w0q[:, kc, c * 128:(c + 1) * 128], lambda kc: x0T[slot][:, kc, :],
                                    [Bc, x0Tb[slot]])
                copy_op(S, "act", qT[slot][:, c, :], pt[:, :], [pb], [qTb[slot]])

        def sb_piece(gi, i):
            if gi >= len(sbs):
                return
            if i == 0:
                xT_blocks(gi, (0, 1))
            elif i == 1:
                xT_blocks(gi, (2, 3))
                k_proj(gi)
            elif i == 2:
                v_proj(gi)
                q_proj(gi, range(0, 4))
            else:
                q_proj(gi, range(4, 8))

        def gate(gi, b):
            if gi >= len(sbs):
                return
            slot = gi % 2
            a = (4 * gi + b) % 2
            for half in range(2):
                pt, pb = proj_group(S, PS, 512, lambda kc: x0T[slot][:, kc, b * 128:(b + 1) * 128],
                                    lambda kc: w0g[:, kc, half * 512:(half + 1) * 512], [Bc, x0Tb[slot]])
                S.op("act", lambda e: e.activation(out=sg[a][:, half * 512:(half + 1) * 512], in_=pt[:, :], func=AF.Tanh, scale=0.5),
                     [pb], [sgb[a]])
                S.op("dve", lambda e: e.scalar_tensor_tensor(out=sg[a][:, half * 512:(half + 1) * 512],
                                                             in0=sg[a][:, half * 512:(half + 1) * 512], scalar=1.0, in1=pt[:, :],
                                                             op0=ALU.add, op1=ALU.mult), [pb, sgb[a]], [sgb[a]])

        def scores(gi, n, b, g, js):
            pp = g % 2
            slot = gi % 2
            groups = [js[0:2]] + ([js[2:3]] if len(js) > 2 else [])
            for grp in groups:
                L = len(grp)
                bk = [PS.next("sc") for _ in range(2)]
                rd = [qTb[slot]]

                def f(e):
                    i = None
                    for idx, j in enumerate(grp):
                        nj = n + j - 1
                        gj = gi + (nj // 4 - n // 4)
                        bb = nj % 4
                        for hf in range(2):
                            i = e.matmul(bk[hf][0][:, idx * 256:(idx + 1) * 256].rearrange("p (a b) -> p a b", a=2),
                                         lhsT=kT[gj % 3][hf * 64:(hf + 1) * 64, g, bb * 128:(bb + 1) * 128],
                                         rhs=qT[slot][hf * 64:(hf + 1) * 64, 2 * g:2 * g + 2, b * 128:(b + 1) * 128],
                                         start=True, stop=True)
                    return i
                for j in grp:
                    rd.append(kTb[(gi + ((n + j - 1) // 4 - n // 4)) % 3])
                S.op("pe", f, rd, [bk[0][1], bk[1][1]])
                j0 = grp[0]
                for hf in range(2):
                    pt, pb = bk[hf]
                    xi = cnt["ex"] % 2
                    cnt["ex"] += 1
                    S.op("act", lambda e: e.activation(out=ex[xi][:, 0:L * 256], in_=pt[:, 0:L * 256], func=AF.Exp, scale=0.125),
                         [pb], [exb[xi]])
                    S.op("dve", lambda e: e.tensor_tensor(
                        out=pT[pp][:, j0:j0 + L, hf * 256:(hf + 1) * 256],
                        in0=ex[xi][:, 0:L * 256].rearrange("p (l c) -> p l c", l=L),
                        in1=etab[:, j0:j0 + L, g, hf * 256:(hf + 1) * 256], op=ALU.mult),
                        [exb[xi], Bc], [pTb[pp][j][hf] for j in grp])

        def pv(gi, n, g, js, a):
            pp = g % 2
            ot, ob = PS.next("o")

            def f(e):
                i = None
                for r in range(4):
                    off = (r % 2) * 256 + (r // 2) * 128
                    for ji, j in enumerate(js):
                        nj = n + j - 1
                        gj = gi + (nj // 4 - n // 4)
                        i = e.matmul(ot[:, r * 128:r * 128 + 65], lhsT=pT[pp][:, j, off:off + 128],
                                     rhs=va[gj % 3][:, nj % 4, g, 0:65], start=(ji == 0), stop=(ji == len(js) - 1))
                return i
            rd = [pTb[pp][j][hf_] for j in js for hf_ in range(2)] + [vab[(gi + ((n + j - 1) // 4 - n // 4)) % 3] for j in js]
            S.op("pe", f, rd, [ob])
            obv = ot[:, :].rearrange("p (r d) -> p r d", r=4)
            dn, rdn, dnb = den[a], rden[a], denb[a][g]
            S.op("dve", lambda e: e.scalar_tensor_tensor(out=dn[:, 4 * g:4 * g + 4], in0=obv[:, :, 64], scalar=2.0,
                                                         in1=esk[:, 4 * g:4 * g + 4], op0=ALU.mult, op1=ALU.add), [ob, Bc], [dnb])
            S.op("dve", lambda e: e.reciprocal(out=rdn[:, 4 * g:4 * g + 4], in_=dn[:, 4 * g:4 * g + 4]), [dnb], [dnb])
            S.op("dve", lambda e: e.tensor_tensor(
                out=on[a][:, g * 256:(g + 1) * 256].rearrange("p (r d) -> p r d", r=4), in0=obv[:, :, 0:64],
                in1=rdn[:, 4 * g:4 * g + 4].unsqueeze(2).to_broadcast([128, 4, 64]), op=ALU.mult), [ob, dnb], [onb[a]])

        def make_unit(gi, b):
            s, k = sbs[gi]
            n = 4 * k + b
            row0 = s * SEQ + n * 128
            js = [j for j in range(3) if 0 <= n + j - 1 < NB]
            a = (4 * gi + b) % 2

            def s0():
                if gi == 0 and b == 0:
                    for i in range(4):
                        sb_piece(0, i)
                    gate(0, 0)
                sb_piece(gi + 1, b)
                if b < 3:
                    gate(gi, b + 1)
                else:
                    gate(gi + 1, 0)
                scores(gi, n, b, 0, js)
                for g in range(4):
                    if g + 1 < 4:
                        scores(gi, n, b, g + 1, js)
                    pv(gi, n, g, js, a)
                S.op("dve", lambda e: e.tensor_tensor(out=onh[a][:, :], in0=on[a][:, :], in1=sg[a][:, :], op=ALU.mult),
                     [onb[a], sgb[a]], [onhb[a]])

            def s1():
                transpose_rows(S, PS, identb, Bc, onh[a], onhb[a], lambda kc0: onT[a][:, kc0:kc0 + 4, :], onTb[a], evac=("act", "act"))
                S.dma("sp", xr[a][:, :], x[row0:row0 + 128, :], writes=[xrb[a]])
                for half in range(2):
                    pt, pb = proj_group(S, PS, 512, lambda kc: onT[a][:, kc, :], lambda kc: w0o[:, kc, half * 512:(half + 1) * 512],
                                        [onTb[a], Bc])
                    S.op("dve", lambda e: e.scalar_tensor_tensor(out=z[a][:, half * 512:(half + 1) * 512],
                                                                 in0=xr[a][:, half * 512:(half + 1) * 512], scalar=DN_ALPHA,
                                                                 in1=pt[:, :], op0=ALU.mult, op1=ALU.add), [pb, xrb[a]], [zb[a]])

            def s2():
                layer_norm_rows(S, z[a], zb[a], lng, lnb, Bc, lnt[a], lntb[a], mhalf=mhalf)
                S.dma("sp", x1s[row0:row0 + 128, :], z[a][:, :], reads=[zb[a]])
            return [s0, s1, s2]

        units = [make_unit(gi, b) for gi in range(len(sbs)) for b in range(4)]
        run_pipeline(units, 3)


def gla_block(S, PS, G, b, chunk_order, cnt):
    at, ab = PS.next("at")

    def f(e):
        i = None
        for h in range(4):
            i = e.matmul(at[:, h * 128:(h + 1) * 128], lhsT=G["ki"][:, h, b * 128:(b + 1) * 128],
                         rhs=G["qdu"][:, h, b * 128:(b + 1) * 128], start=True, stop=True)
        return i
    S.op("pe", f, [G["kib"], G["qdub"]], [ab])
    attm, attmb = G["attm"], G["attmb"]
    S.op("dve", lambda e: e.tensor_tensor(out=attm[:, :, :], in0=at[:, :].rearrange("p (h t) -> p h t", h=4),
                                          in1=G["trim"][:, :].unsqueeze(1).to_broadcast([128, 4, 128]), op=ALU.mult),
         [ab, G["Bc"]], [attmb])
    ob = [PS.next("o") for _ in range(2)]
    obufs = [x_[1] for x_ in ob]
    zr = G["zeros"]

    def f(e):
        i = None
        for i2 in range(2):
            e.matmul(ob[i2][0][:, :], lhsT=zr[0:1, 0:128], rhs=zr[0:1, 0:512], start=True, stop=False)
        for h in range(4):
            i = e.matmul(ob[h // 2][0][:, (h % 2) * 256:(h % 2 + 1) * 256], lhsT=attm[:, h, :],
                         rhs=G["v"][:, b, h * 256:(h + 1) * 256], start=False, stop=False)
        return i
    S.op("pe", f, [attmb, G["vb"], G["Bc"]], obufs)
    St, Stb, Sbf, Sbfb = G["St"], G["Stb"], G["Sbf"], G["Sbfb"]
    for ci, cl in enumerate(chunk_order):
        c = 2 * b + cl
        sp = cnt["s"] % 2
        last = ci == 1

        def f(e):
            i = None
            for h in range(4):
                i = e.matmul(ob[h // 2][0][:, (h % 2) * 256:(h % 2 + 1) * 256], lhsT=G["qdp"][:, h, c, :], rhs=Sbf[sp][:, h, :],
                             start=False, stop=(last and h % 2 == 1))
            return i
        S.op("pe", f, [G["qdpb"], Sbfb[sp]], obufs)
        if G.get("u_split"):
            for hp in range(2):
                u, ubb = PS.next("u")

                def f(e):
                    i = None
                    for h in (2 * hp, 2 * hp + 1):
                        i = e.matmul(u[:, (h % 2) * 256:(h % 2 + 1) * 256],
                                     lhsT=G["kt"][cl * 64:(cl + 1) * 64, b, h * 128:(h + 1) * 128],
                                     rhs=G["v"][cl * 64:(cl + 1) * 64, b, h * 256:(h + 1) * 256], start=True, stop=True)
                    return i
                S.op("pe", f, [G["ktb"], G["vb"]], [ubb])
                for h in (2 * hp, 2 * hp + 1):
                    S.op("dve", lambda e: e.scalar_tensor_tensor(out=St[:, h, :], in0=St[:, h, :], scalar=G["ecl"][:, h, c:c + 1],
                                                                 in1=u[:, (h % 2) * 256:(h % 2 + 1) * 256], op0=ALU.mult, op1=ALU.add),
                         [Stb[h], ubb, G["eclb"]], [Stb[h]])
        else:
            ub = [PS.next("u") for _ in range(2)]

            def f(e):
                i = None
                for h in range(4):
                    i = e.matmul(ub[h // 2][0][:, (h % 2) * 256:(h % 2 + 1) * 256],
                                 lhsT=G["kt"][cl * 64:(cl + 1) * 64, b, h * 128:(h + 1) * 128],
                                 rhs=G["v"][cl * 64:(cl + 1) * 64, b, h * 256:(h + 1) * 256], start=True, stop=True)
                return i
            S.op("pe", f, [G["ktb"], G["vb"]], [ub[0][1], ub[1][1]])
            for h in range(4):
                u, ubb = ub[h // 2]
                S.op("dve", lambda e: e.scalar_tensor_tensor(out=St[:, h, :], in0=St[:, h, :], scalar=G["ecl"][:, h, c:c + 1],
                                                             in1=u[:, (h % 2) * 256:(h % 2 + 1) * 256], op0=ALU.mult, op1=ALU.add),
                     [Stb[h], ubb, G["eclb"]], [Stb[h]])
        cnt["s"] += 1
        sn = cnt["s"] % 2
        S.op("act", lambda e: e.copy(out=Sbf[sn][:, :, :], in_=St[:, :, :]), Stb, [Sbfb[sn]])
    return ob


def phase_a2(nc, S, PS, W, x1s, ofs, sgs, vs, qdbs, kibs, ktbs, ecls, nseq):
    PS.config({"at": [0], "o": [1, 2], "u": [3], "mm": [4, 5, 6, 7]})
    with ExitStack() as ph:
        def T(n, sh, dt):
            return ph.enter_context(nc.sbuf_tensor(n, sh, dt))
        w1q = T("w1q", [128, 8, 512], BF16)
        w1k = T("w1k", [128, 8, 512], BF16)
        w1lr = T("w1lr", [128, 8, 64], BF16)
        w1v = T("w1v", [128, 8, 1024], BF16)
        w1g = T("w1g", [128, 8, 1024], BF16)
        wgt = T("wgt", [64, 512], BF16)
        nbias = T("nbias", [128, 2, 4], F32)
        hn = T("hn", [128, 1024], F32)
        ident = T("ident2", [128, 128], F32)
        identb = T("identb2", [128, 128], BF16)
        smask = T("smask", [128, 512], F32)
        trif = T("trif", [128, 128], F32)
        zeros = T("zeros2", [1, 512], BF16)
        Bc = Buf(const=True)
        S.dma("sp", ident[:, :], W["c_ident"][:, :], writes=[Bc])
        S.op("dve", lambda e: e.tensor_copy(out=identb[:, :], in_=ident[:, :]), [Bc], [Bc])
        S.dma("sp", smask[:, :], W["c_scanmask"][:, :], writes=[Bc])
        S.dma("sp", trif[:, :], W["c_trif"][:, :], writes=[Bc])
        S.dma("sp", hn[:, :], W["l1_head_norm"].partition_broadcast(128), writes=[Bc])
        S.dma("sp", nbias[:, 0, :], W["l1_b_gate_f"][:, :], writes=[Bc])
        S.dma("sp", nbias[:, 1, :], W["l1_b_gate_b"][:, :], writes=[Bc])
        S.op("dve", lambda e: e.tensor_scalar(out=nbias[:, :, :], in0=nbias[:, :, :], scalar1=-1.0, scalar2=None, op0=ALU.mult),
             [Bc], [Bc])
        S.op("dve", lambda e: e.memset(w1lr[:, :, :], 0.0), [], [Bc])
        S.op("dve", lambda e: e.memset(wgt[:, :], 0.0), [], [Bc])
        S.op("dve", lambda e: e.memset(zeros[:, :], 0.0), [], [Bc])
        with ExitStack() as st:
            NS_ = 4
            stg = [st.enter_context(nc.sbuf_tensor("stgb%d" % i, [128, 3104], F32)) for i in range(NS_)]
            Bs = bufs(NS_)
            for kc in range(8):
                i = kc % NS_
                S.dma("sp", stg[i][:, :], W["l1_w_in"][kc * 128:(kc + 1) * 128, :], writes=[Bs[i]])
                copy_op(S, "dve", w1q[:, kc, :], stg[i][:, 0:512], [Bs[i]], [Buf()])
                copy_op(S, "pool", w1k[:, kc, :], stg[i][:, 512:1024], [Bs[i]], [Buf()])
                copy_op(S, "act", w1v[:, kc, :], stg[i][:, 1024:2048], [Bs[i]], [Buf()])
                copy_op(S, "dve", w1g[:, kc, :], stg[i][:, 2048:3072], [Bs[i]], [Buf()])
                copy_op(S, "pool", w1lr[:, kc, 0:16], stg[i][:, 3072:3088], [Bs[i], Bc], [Buf()])
                copy_op(S, "pool", w1lr[:, kc, 32:48], stg[i][:, 3088:3104], [Bs[i], Bc], [Buf()])
            gst = st.enter_context(nc.sbuf_tensor("stgg", [64, 512], F32))
            Bg = Buf()
            S.dma("sp", gst[0:16, :], W["l1_w_gate_f"][:, :], writes=[Bg])
            S.dma("sp", gst[32:48, :], W["l1_w_gate_b"][:, :], writes=[Bg])
            copy_op(S, "dve", wgt[0:16, :], gst[0:16, :], [Bg, Bc], [Buf()])
            copy_op(S, "dve", wgt[32:48, :], gst[32:48, :], [Bg, Bc], [Buf()])
            S.barrier()
        xl = [T("xl%d" % i, [128, 1024], F32) for i in range(2)]
        xlb = bufs(2)
        xlh = [T("xlh%d" % i, [128, 1024], BF16) for i in range(2)]
        xlhb = bufs(2)
        x1T_ = [T("x1T%d" % i, [128, 8, 512], BF16) for i in range(2)]
        x1Tb_ = bufs(2)
        lrT = T("lrT", [64, 512], BF16)
        lrTb = Buf()
        v1 = [T("v1_%d" % i, [128, 4, 1024], BF16) for i in range(2)]
        v1b = bufs(2)
        sg1 = [T("sg1_%d" % i, [128, 1024], F32) for i in range(2)]
        sg1b = bufs(2)
        qf_ = [T("qf%d" % i, [128, 512], F32) for i in range(2)]
        kf_ = [T("kf%d" % i, [128, 512], F32) for i in range(2)]
        qfb_, kfb_ = bufs(2), bufs(2)
        Pt = [T("Pt%d" % i, [128, 520], F32) for i in range(2)]
        Ptb = bufs(2)
        Pd = [T("Pd%d" % i, [128, 512], F32) for i in range(2)]
        Pdb = bufs(2)
        tp = [T("tp%d" % i, [128, 8], F32) for i in range(2)]
        eb = [T("eb%d" % i, [128, 512], F32) for i in range(2)]
        ebb = bufs(2)
        enb = [T("enb%d" % i, [128, 512], F32) for i in range(2)]
        enbb = bufs(2)
        eclf = [T("eclf%d" % i, [128, 4, 8], F32) for i in range(2)]
        eclfb = bufs(2)
        qduf = [T("qduf%d" % i, [128, 4, 512], BF16) for i in range(2)]
        qdufb = bufs(2)
        qdp = [T("qdp%d" % i, [128, 4, 8, 128], BF16) for i in range(2)]
        qdpb = bufs(2)
        kif = [T("kif%d" % i, [128, 4, 512], BF16) for i in range(2)]
        kifb = bufs(2)
        ktf = [T("ktf%d" % i, [128, 4, 512], BF16) for i in range(2)]
        ktfb = bufs(2)
        eclB = T("eclB", [128, 4, 8], F32)
        eclBb = Buf()
        qduB = T("qduB", [128, 4, 512], BF16)
        qduBb = Buf()
        kiB = T("kiB", [128, 4, 512], BF16)
        kiBb = Buf()
        ktB = T("ktB", [128, 4, 512], BF16)
        ktBb = Buf()
        ktT_ = [T("ktT%d" % i, [128, 512], BF16) for i in range(2)]
        ktTb_ = bufs(2)
        attm = T("attm", [128, 4, 128], BF16)
        attmb = Buf()
        St = T("St", [128, 4, 256], F32)
        Stb = bufs(4)
        Sbf = [T("Sbf%d" % i, [128, 4, 256], BF16) for i in range(2)]
        Sbfb = bufs(2)
        of = [T("of%d" % i, [128, 1024], F32) for i in range(2)]
        ofb = bufs(2)
        for i in range(2):
            S.op("pool", lambda e: e.memset(qdp[i][:, :, :, :], 0.0), [], [qdpb[i]])
            S.op("pool", lambda e: e.memset(Pt[i][:, :], 0.0), [], [Ptb[i]])
        cnt = {"s": 0, "blk": 0, "g": 0}

        def prep_pieces(s, k):
            sb = s * NSB + k
            tok0 = s * SEQ + k * 512
            p = k % 2
            x1T, x1Tb = x1T_[p], x1Tb_[p]

            def piece0():
                for b in range(4):
                    a = cnt["blk"] % 2
                    cnt["blk"] += 1
                    S.dma("sp", xl[a][:, :], x1s[tok0 + b * 128:tok0 + (b + 1) * 128, :], writes=[xlb[a]])
                    S.op("act", lambda e: e.copy(out=xlh[a][:, :], in_=xl[a][:, :]), [xlb[a]], [xlhb[a]])
                    transpose_rows(S, PS, identb, Bc, xlh[a], xlhb[a], lambda kc0: x1T[:, kc0:kc0 + 4, b * 128:(b + 1) * 128], x1Tb,
                                   evac=("act", "act"))
                pt, pb = proj_group(S, PS, 512, lambda kc: w1lr[:, kc, :], lambda kc: x1T[:, kc, :], [Bc, x1Tb], rows=64)
                copy_op(S, "act", lrT[:, :], pt[0:64, :], [pb], [lrTb])
                for b in range(4):
                    for half in range(2):
                        pt, pb = proj_group(S, PS, 512, lambda kc: x1T[:, kc, b * 128:(b + 1) * 128],
                                            lambda kc: w1v[:, kc, half * 512:(half + 1) * 512], [Bc, x1Tb])
                        copy_op(S, "act", v1[p][:, b, half * 512:(half + 1) * 512], pt[:, :], [pb], [v1b[p]])
                S.dma("sp", vs[tok0:tok0 + 512, :].rearrange("(b p) c -> p b c", p=128), v1[p][:, :, :], reads=[v1b[p]])

            def piece_gate():
                for b in range(4):
                    a = cnt["g"] % 2
                    cnt["g"] += 1
                    for half in range(2):
                        pt, pb = proj_group(S, PS, 512, lambda kc: x1T[:, kc, b * 128:(b + 1) * 128],
                                            lambda kc: w1g[:, kc, half * 512:(half + 1) * 512], [Bc, x1Tb])
                        S.op("act", lambda e: e.activation(out=sg1[a][:, half * 512:(half + 1) * 512], in_=pt[:, :], func=AF.Silu),
                             [pb], [sg1b[a]])
                    S.op("dve", lambda e: e.tensor_tensor(out=sg1[a][:, :], in0=sg1[a][:, :], in1=hn[:, :], op=ALU.mult),
                         [sg1b[a], Bc], [sg1b[a]])
                    S.dma("sp", sgs[tok0 + b * 128:tok0 + (b + 1) * 128, :], sg1[a][:, :], reads=[sg1b[a]])

            def piece_head(h):
                qf, kf, qfb, kfb = qf_[h % 2], kf_[h % 2], qfb_[h % 2], kfb_[h % 2]

                def fn():
                    for (wt_, dst, dstb) in ((w1q, qf, qfb), (w1k, kf, kfb)):
                        pt, pb = proj_group(S, PS, 512, lambda kc: wt_[:, kc, h * 128:(h + 1) * 128], lambda kc: x1T[:, kc, :],
                                            [Bc, x1Tb])
                        copy_op(S, "act", dst[:, :], pt[:, :], [pb], [dstb])
                    for d in range(2):
                        ktT, ktTb = ktT_[d], ktTb_[d]
                        rows = slice(0, 16) if d == 0 else slice(32, 48)
                        ecl_, eclb_ = (eclf[p], eclfb[p]) if d == 0 else (eclB, eclBb)
                        qdu_, qdub_ = (qduf[p], qdufb[p]) if d == 0 else (qduB, qduBb)
                        ki_, kib_ = (kif[p], kifb[p]) if d == 0 else (kiB, kiBb)
                        kt_, ktb_ = (ktf[p], ktfb[p]) if d == 0 else (ktB, ktBb)
                        pt, pb = PS.next("mm")
                        S.op("pe", lambda e: e.matmul(pt[:, :], lhsT=wgt[rows, h * 128:(h + 1) * 128], rhs=lrT[rows, :],
                                                      start=True, stop=True), [Bc, lrTb], [pb])
                        P_ = Pt[d]
                        S.op("act", lambda e: e.activation(out=P_[:, 1:513], in_=pt[:, :], func=AF.Exp, scale=-1.0,
                                                           bias=nbias[:, d, h:h + 1]), [pb, Bc], [Ptb[d]])
                        S.op("act", lambda e: e.activation(out=P_[:, 1:513], in_=P_[:, 1:513], func=AF.Ln, bias=1.0),
                             [Ptb[d]], [Ptb[d]])
                        Pv = Pd[d][:, :].rearrange("p (c t) -> p c t", c=8)
                        if d == 0:
                            S.op("dve", lambda e: e.tensor_tensor_scan(out=Pd[d][:, :], data0=smask[:, :], data1=P_[:, 1:513],
                                                                       initial=0.0, op0=ALU.mult, op1=ALU.add),
                                 [Ptb[d], Bc], [Pdb[d]])
                            plast = Pv[:, :, 63]
                        else:
                            S.op("dve", lambda e: e.tensor_tensor_scan(out=Pd[d][:, :], data0=P_[:, 0:512], data1=smask[:, :],
                                                                       initial=0.0, op0=ALU.add, op1=ALU.mult),
                                 [Ptb[d], Bc], [Pdb[d]])
                            S.op("dve", lambda e: e.tensor_tensor(
                                out=tp[d][:, :], in0=Pv[:, :, 63],
                                in1=P_[:, 1:513].rearrange("p (c t) -> p c t", c=8)[:, :, 63], op=ALU.add),
                                [Pdb[d], Ptb[d]], [Pdb[d]])
                            S.op("dve", lambda e: e.tensor_tensor(
                                out=Pv, in0=tp[d][:, :].unsqueeze(2).to_broadcast([128, 8, 64]), in1=Pv, op=ALU.subtract),
                                [Pdb[d]], [Pdb[d]])
                            plast = tp[d][:, :]
                        S.op("act", lambda e: e.activation(out=ecl_[:, h, :], in_=plast, func=AF.Exp, scale=-1.0 / 16),
                             [Pdb[d]], [eclb_])
                        S.op("act", lambda e: e.activation(out=eb[d][:, :], in_=Pd[d][:, :], func=AF.Exp, scale=-1.0 / 16),
                             [Pdb[d]], [ebb[d]])
                        S.op("act", lambda e: e.activation(out=enb[d][:, :], in_=Pd[d][:, :], func=AF.Exp, scale=1.0 / 16),
                             [Pdb[d]], [enbb[d]])
                        S.op("dve", lambda e: e.scalar_tensor_tensor(out=qdu_[:, h, :], in0=qf[:, :], scalar=QSCALE1, in1=eb[d][:, :],
                                                                     op0=ALU.mult, op1=ALU.mult), [qfb, ebb[d]], [qdub_])
                        S.op("dve", lambda e: e.tensor_tensor(out=ki_[:, h, :], in0=kf[:, :], in1=enb[d][:, :], op=ALU.mult),
                             [kfb, enbb[d]], [kib_])
                        env = enb[d][:, :].rearrange("p (c t) -> p c t", c=8)
                        S.op("dve", lambda e: e.tensor_tensor(
                            out=env, in0=env, in1=ecl_[:, h, :].unsqueeze(2).to_broadcast([128, 8, 64]), op=ALU.mult),
                            [enbb[d], eclb_], [enbb[d]])
                        S.op("dve", lambda e: e.tensor_tensor(out=ktT[:, :], in0=kf[:, :], in1=enb[d][:, :], op=ALU.mult),
                             [kfb, enbb[d]], [ktTb])
                        pt2, pb2 = PS.next("mm")
                        pt2b = pt2[:, :].bitcast(BF16)

                        def f(e):
                            i = None
                            for b in range(4):
                                i = e.transpose(out=pt2b[:, b * 128:(b + 1) * 128], in_=ktT[:, b * 128:(b + 1) * 128], identity=identb[:, :])
                            return i
                        S.op("pe", f, [ktTb, Bc], [pb2])
                        copy_op(S, "act", kt_[:, :, h * 128:(h + 1) * 128], pt2b[:, 0:512].rearrange("p (b k) -> p b k", b=4),
                                [pb2], [ktb_])
                    for par in range(2):
                        S.op("act", lambda e: e.copy(
                            out=qdp[p][:, h, par::2, par * 64:(par + 1) * 64],
                            in_=qduf[p][:, h, :].rearrange("p (b c t) -> p b c t", b=4, c=2)[:, :, par, :]), [qdufb[p]], [qdpb[p]])
                    if h == 3:
                        S.dma("sp", qdbs[sb, :, :], qduB[:, :, :].rearrange("p h t -> p (h t)"), reads=[qduBb])
                        S.dma("sp", kibs[sb, :, :], kiB[:, :, :].rearrange("p h t -> p (h t)"), reads=[kiBb])
                        S.dma("sp", ktbs[tok0:tok0 + 512, :].rearrange("(b p) c -> p b c", p=128), ktB[:, :, :], reads=[ktBb])
                        S.dma("sp", ecls[sb, :, :], eclB[:, :, :].rearrange("p h c -> p (h c)"), reads=[eclBb])
                return fn
            return [piece0, piece_gate, piece_head(0), piece_head(1), piece_head(2), piece_head(3)]

        def gla_pieces(s, k):
            tok0 = s * SEQ + k * 512
            p = k % 2
            G = dict(ki=kif[p], kib=kifb[p], qdu=qduf[p], qdub=qdufb[p], qdp=qdp[p], qdpb=qdpb[p], kt=ktf[p], ktb=ktfb[p],
                     v=v1[p], vb=v1b[p], ecl=eclf[p], eclb=eclfb[p], St=St, Stb=Stb, Sbf=Sbf, Sbfb=Sbfb, attm=attm, attmb=attmb,
                     trim=trif, Bc=Bc, zeros=zeros, u_split=True)

            def blk(b):
                def fn():
                    if k == 0 and b == 0:
                        S.op("dve", lambda e: e.memset(St[:, :, :], 0.0), [], Stb)
                        S.op("dve", lambda e: e.memset(Sbf[cnt["s"] % 2][:, :, :], 0.0), [], [Sbfb[cnt["s"] % 2]])
                    ob = gla_block(S, PS, G, b, (0, 1), cnt)
                    a = (4 * k + b) % 2
                    for i2 in range(2):
                        copy_op(S, ("act", "dve")[i2], of[a][:, i2 * 512:(i2 + 1) * 512], ob[i2][0][:, :], [ob[i2][1]], [ofb[a]])
                    S.dma("sp", ofs[tok0 + b * 128:tok0 + (b + 1) * 128, :], of[a][:, :], reads=[ofb[a]])
                return fn
            return [blk(0), blk(1), blk(2), blk(3)]

        sbs = [(s, k) for s in range(nseq) for k in range(NSB)]
        for fn in prep_pieces(*sbs[0]):
            fn()
        for i, (s, k) in enumerate(sbs):
            gp = gla_pieces(s, k)
            pp_ = prep_pieces(*sbs[i + 1]) if i + 1 < len(sbs) else []
            order = [gp[0]] + pp_[0:1] + [gp[1]] + pp_[1:3] + [gp[2]] + pp_[3:5] + [gp[3]] + pp_[5:6]
            for fn in order:
                fn()


def phase_b(nc, S, PS, W, x1s, ofs, sgs, vs, qdbs, kibs, ktbs, ecls, y, nseq):
    PS.config({"at": [0], "o": [1, 2], "u": [3, 4], "mm": [5, 6, 7]})
    with ExitStack() as ph:
        def T(n, sh, dt):
            return ph.enter_context(nc.sbuf_tensor(n, sh, dt))
        w1o = T("w1o", [128, 8, 1024], BF16)
        lng = T("lng1", [128, 1024], F32)
        lnb = T("lnb1", [128, 1024], F32)
        ident = T("ident3", [128, 128], F32)
        identb = T("identb3", [128, 128], BF16)
        trib = T("trib", [128, 128], F32)
        zeros = T("zeros3", [1, 512], BF16)
        Bc = Buf(const=True)
        S.dma("sp", ident[:, :], W["c_ident"][:, :], writes=[Bc])
        S.op("dve", lambda e: e.tensor_copy(out=identb[:, :], in_=ident[:, :]), [Bc], [Bc])
        S.dma("sp", trib[:, :], W["c_trib"][:, :], writes=[Bc])
        S.dma("sp", lng[:, :], W["l1_ln_g"].partition_broadcast(128), writes=[Bc])
        S.dma("sp", lnb[:, :], W["l1_ln_b"].partition_broadcast(128), writes=[Bc])
        S.op("dve", lambda e: e.memset(zeros[:, :], 0.0), [], [Bc])
        with ExitStack() as st:
            NS_ = 4
            stg = [st.enter_context(nc.sbuf_tensor("stgc%d" % i, [128, 1024], F32)) for i in range(NS_)]
            Bs = bufs(NS_)
            for kc in range(8):
                i = kc % NS_
                S.dma("sp", stg[i][:, :], W["l1_w_out"][kc * 128:(kc + 1) * 128, :], writes=[Bs[i]])
                copy_op(S, ("dve", "act")[kc % 2], w1o[:, kc, :], stg[i][:, :], [Bs[i]], [Buf()])
            S.barrier()
        qdu = [T("bqdu%d" % i, [128, 4, 512], BF16) for i in range(2)]
        qdub = bufs(2)
        qdp = [T("bqdp%d" % i, [128, 4, 8, 128], BF16) for i in range(2)]
        qdpb = bufs(2)
        ki = [T("bki%d" % i, [128, 4, 512], BF16) for i in range(2)]
        kib_ = bufs(2)
        kt = [T("bkt%d" % i, [128, 4, 512], BF16) for i in range(2)]
        ktb_ = bufs(2)
        v = [T("bv%d" % i, [128, 4, 1024], BF16) for i in range(2)]
        vb_ = bufs(2)
        ecl = [T("becl%d" % i, [128, 4, 8], F32) for i in range(2)]
        eclb_ = bufs(2)
        R3 = 3
        sgh = [T("bsg%d" % i, [128, 1024], F32) for i in range(R3)]
        sghb = bufs(R3)
        xl = [T("bx%d" % i, [128, 1024], F32) for i in range(R3)]
        xlb = bufs(R3)
        ofl = [T("bof%d" % i, [128, 1024], F32) for i in range(R3)]
        oflb = bufs(R3)
        attm = T("battm", [128, 4, 128], BF16)
        attmb = Buf()
        St = T("bSt", [128, 4, 256], F32)
        Stb = bufs(4)
        Sbf = [T("bSbf%d" % i, [128, 4, 256], BF16) for i in range(2)]
        Sbfb = bufs(2)
        osum_ = [T("osum%d" % i, [128, 1024], F32) for i in range(2)]
        osumb_ = bufs(2)
        junk_ = [T("junk%d" % i, [128, 4, 256], F32) for i in range(2)]
        ss_ = [T("ss%d" % i, [128, 8], F32) for i in range(2)]
        ssb_ = bufs(2)
        on_ = [T("bon%d" % i, [128, 1024], BF16) for i in range(2)]
        onb_ = bufs(2)
        onT_ = [T("bonT%d" % i, [128, 8, 128], BF16) for i in range(2)]
        onTb_ = bufs(2)
        z = [T("bz%d" % i, [128, 1024], F32) for i in range(2)]
        zb = bufs(2)
        lnt = [(T("blnst%d" % i, [128, 2, 6], F32), T("blnmv%d" % i, [128, 2], F32), T("blnve%d" % i, [128, 4], F32)) for i in range(2)]
        lntb = bufs(2)
        for i in range(2):
            S.op("pool", lambda e: e.memset(qdp[i][:, :, :, :], 0.0), [], [qdpb[i]])
        cnt = {"s": 0}
        obs = {}

        def make_unit(ui, s, k, b, p):
            sb = s * NSB + k
            tok0 = s * SEQ + k * 512
            r0 = tok0 + b * 128
            a3 = ui % R3
            a2 = ui % 2
            osum, osumb, junk, ss, ssb = osum_[a2], osumb_[a2], junk_[a2], ss_[a2], ssb_[a2]
            on, onb, onT, onTb = on_[a2], onb_[a2], onT_[a2], onTb_[a2]
            G = dict(ki=ki[p], kib=kib_[p], qdu=qdu[p], qdub=qdub[p], qdp=qdp[p], qdpb=qdpb[p], kt=kt[p], ktb=ktb_[p],
                     v=v[p], vb=vb_[p], ecl=ecl[p], eclb=eclb_[p], St=St, Stb=Stb, Sbf=Sbf, Sbfb=Sbfb, attm=attm, attmb=attmb,
                     trim=trib, Bc=Bc, zeros=zeros)

            def s0():
                if b == 3:
                    S.dma("sp", qdu[p][:, :, :].rearrange("p h t -> p (h t)"), qdbs[sb, :, :], writes=[qdub[p]])
                    S.dma("sp", ki[p][:, :, :].rearrange("p h t -> p (h t)"), kibs[sb, :, :], writes=[kib_[p]])
                    S.dma("sp", kt[p][:, :, :], ktbs[tok0:tok0 + 512, :].rearrange("(b p) c -> p b c", p=128), writes=[ktb_[p]])
                    S.dma("sp", v[p][:, :, :], vs[tok0:tok0 + 512, :].rearrange("(b p) c -> p b c", p=128), writes=[vb_[p]])
                    S.dma("sp", ecl[p][:, :, :].rearrange("p h c -> p (h c)"), ecls[sb, :, :], writes=[eclb_[p]])
                    for h in range(4):
                        for par in range(2):
                            S.op("act", lambda e: e.copy(
                                out=qdp[p][:, h, par::2, par * 64:(par + 1) * 64],
                                in_=qdu[p][:, h, :].rearrange("p (b c t) -> p b c t", b=4, c=2)[:, :, par, :]), [qdub[p]], [qdpb[p]])
                    if k == NSB - 1:
                        S.op("dve", lambda e: e.memset(St[:, :, :], 0.0), [], Stb)
                        S.op("dve", lambda e: e.memset(Sbf[cnt["s"] % 2][:, :, :], 0.0), [], [Sbfb[cnt["s"] % 2]])
                S.dma("sp", sgh[a3][:, :], sgs[r0:r0 + 128, :], writes=[sghb[a3]])
                S.dma("sp", xl[a3][:, :], x1s[r0:r0 + 128, :], writes=[xlb[a3]])
                S.dma("sp", ofl[a3][:, :], ofs[r0:r0 + 128, :], writes=[oflb[a3]])
                obs[ui] = gla_block(S, PS, G, b, (1, 0), cnt)

            def s1():
                ob = obs.pop(ui)
                for i2 in range(2):
                    S.op("dve", lambda e: e.tensor_tensor(out=osum[:, i2 * 512:(i2 + 1) * 512], in0=ob[i2][0][:, :],
                                                          in1=ofl[a3][:, i2 * 512:(i2 + 1) * 512], op=ALU.add),
                         [ob[i2][1], oflb[a3]], [osumb])

                def f(e):
                    i = None
                    for h in range(4):
                        i = e.activation(out=junk[:, h, :], in_=osum[:, h * 256:(h + 1) * 256], func=AF.Square,
                                         accum_out=ss[:, h:h + 1])
                    return i
                S.op("act", f, [osumb], [ssb])
                S.op("dve", lambda e: e.tensor_scalar(out=ss[:, 0:4], in0=ss[:, 0:4], scalar1=1.0 / 256, scalar2=RMS_EPS,
                                                      op0=ALU.mult, op1=ALU.add), [ssb], [ssb])
                S.op("act", lambda e: e.activation(out=ss[:, 0:4], in_=ss[:, 0:4], func=AF.Ln), [ssb], [ssb])
                S.op("act", lambda e: e.activation(out=ss[:, 4:8], in_=ss[:, 0:4], func=AF.Exp, scale=-0.5), [ssb], [ssb])
                S.op("dve", lambda e: e.tensor_tensor(out=sgh[a3][:, :], in0=sgh[a3][:, :], in1=osum[:, :], op=ALU.mult),
                     [osumb, sghb[a3]], [sghb[a3]])

                def f(e):
                    i = None
                    for h in range(4):
                        i = e.activation(out=on[:, h * 256:(h + 1) * 256], in_=sgh[a3][:, h * 256:(h + 1) * 256], func=AF.Copy,
                                         scale=ss[:, 4 + h:5 + h])
                    return i
                S.op("act", f, [ssb, sghb[a3]], [onb])
                transpose_rows(S, PS, identb, Bc, on, onb, lambda kc0: onT[:, kc0:kc0 + 4, :], onTb, evac=("act", "dve"))
                for half in range(2):
                    pt, pb = proj_group(S, PS, 512, lambda kc: onT[:, kc, :], lambda kc: w1o[:, kc, half * 512:(half + 1) * 512],
                                        [onTb, Bc])
                    S.op("dve", lambda e: e.scalar_tensor_tensor(out=z[a2][:, half * 512:(half + 1) * 512],
                                                                 in0=xl[a3][:, half * 512:(half + 1) * 512], scalar=DN_ALPHA,
                                                                 in1=pt[:, :], op0=ALU.mult, op1=ALU.add), [pb, xlb[a3]], [zb[a2]])

            def s2():
                layer_norm_rows(S, z[a2], zb[a2], lng, lnb, Bc, lnt[a2], lntb[a2],
                                junk=junk[:, :, :].rearrange("p a b -> p (a b)"), junkb=ssb)
                S.dma("sp", y[r0:r0 + 128, :], z[a2][:, :], reads=[zb[a2]])
            return [s0, s1, s2]

        units = []
        ui = 0
        sbi = 0
        for s in range(nseq):
            for k in range(NSB - 1, -1, -1):
                p = sbi % 2
                sbi += 1
                for b in range(3, -1, -1):
                    units.append(make_unit(ui, s, k, b, p))
                    ui += 1
        run_pipeline(units, 3)


def host_consts():
    H = 16
    slopes = (2.0 ** (-8.0 * np.arange(1, H + 1) / H)).astype(np.float64)
    kl = np.arange(128)[:, None].astype(np.float64)
    ql = np.arange(128)[None, :].astype(np.float64)
    etab = np.zeros((128, 3, 4, 4, 128), np.float64)
    order = [0, 2, 1, 3]
    for j in range(3):
        if j == 0:
            dist = 128 + ql - kl
            ok = kl >= ql
        elif j == 1:
            dist = np.abs(ql - kl)
            ok = np.ones((128, 128), bool)
        else:
            dist = 128 + kl - ql
            ok = kl <= ql
        for g in range(4):
            for pos, r in enumerate(order):
                h = 4 * g + r
                etab[:, j, g, pos, :] = np.where(ok, np.exp(-slopes[h] * dist), 0.0)
    t = np.arange(512)
    scanmask = np.broadcast_to((t % 64 != 0).astype(np.float32)[None, :], (128, 512)).copy()
    s_ = np.arange(128)[:, None]
    t_ = np.arange(128)[None, :]
    same = (s_ // 64) == (t_ // 64)
    trif = (same & (t_ >= s_)).astype(np.float32)
    trib = (same & (t_ < s_)).astype(np.float32)
    return {
        "c_ident": np.eye(128, dtype=np.float32),
        "c_etab": etab.reshape(128, -1).astype(np.float32),
        "c_scanmask": scanmask,
        "c_trif": trif,
        "c_trib": trib,
    }


_NC_CACHE = {}


def kernel(x_prompt, x_sample, l0_w_in, l0_sink, l0_w_out, l0_ln_g, l0_ln_b,
           l1_w_in, l1_w_gate_f, l1_b_gate_f, l1_w_gate_b, l1_b_gate_b, l1_head_norm,
           l1_w_out, l1_ln_g, l1_ln_b):
    f32 = lambda a: np.ascontiguousarray(np.asarray(a, dtype=np.float32))
    xp = f32(x_prompt).reshape(-1, SEQ, D)
    xs = f32(x_sample).reshape(-1, SEQ, D)
    nP = xp.shape[0]
    xall = np.concatenate([xp, xs], axis=0)
    shared = {
        "l0_w_in": f32(l0_w_in), "l0_sink": f32(l0_sink), "l0_w_out": f32(l0_w_out), "l0_ln_g": f32(l0_ln_g), "l0_ln_b": f32(l0_ln_b),
        "l1_w_in": f32(l1_w_in), "l1_w_gate_f": f32(l1_w_gate_f),
        "l1_b_gate_f": f32(np.asarray(l1_b_gate_f).reshape(4, 128).T),
        "l1_w_gate_b": f32(l1_w_gate_b),
        "l1_b_gate_b": f32(np.asarray(l1_b_gate_b).reshape(4, 128).T),
        "l1_head_norm": f32(l1_head_norm), "l1_w_out": f32(l1_w_out), "l1_ln_g": f32(l1_ln_g), "l1_ln_b": f32(l1_ln_b),
    }
    shared.update(host_consts())
    if "nc" not in _NC_CACHE:
        _NC_CACHE["nc"] = build()
    nc = _NC_CACHE["nc"]
    in_maps = []
    for c in range(NCORES):
        m = dict(shared)
        m["x"] = np.ascontiguousarray(xall[c * NSEQ:(c + 1) * NSEQ].reshape(NSEQ * SEQ, D))
        in_maps.append(m)
    res = run_bass_kernel_spmd(nc, in_maps, core_ids=list(range(NCORES)))
    yall = np.stack([np.asarray(r["y"], dtype=np.float32).reshape(NSEQ, SEQ, D) for r in res.results], axis=0)
    yall = yall.reshape(NCORES * NSEQ, SEQ, D)
    return (np.ascontiguousarray(yall[:nP]), np.ascontiguousarray(yall[nP:]))
```
